# Optimizing a Trainium2 kernel written in Bass

```python
import jax, jax.numpy as jnp
from jax import lax
import numpy as np

D_MODEL = 2048
BATCH = 4
SEQ = 2048
DEPTH = 4

N_META = 16
BLOCK = 128
EPS = 1e-6
NEG_INF = -1e30
ROPE_THETA = 10000.0
MLA_HEADS = 8
MLA_Q_RANK = 512
MLA_KV_RANK = 512
MLA_NOPE = 128
MLA_ROPE = 64
MLA_V = 128
CONV_WIDTH = 1024
CONV_K = 3
FOX_HEADS = 8
FOX_HEAD_DIM = 128
FORGET_BIAS_MEAN = 2.0
N_BRANCH = 3
BRANCH_WIDTH = 1024
D_FF = -(-8 * D_MODEL // (3 * 256)) * 256
FOX_WIDTH = FOX_HEADS * FOX_HEAD_DIM
IN_SPLITS = (MLA_Q_RANK, MLA_KV_RANK, MLA_ROPE,
             CONV_WIDTH, CONV_WIDTH, CONV_WIDTH,
             FOX_WIDTH, FOX_WIDTH, FOX_WIDTH, FOX_HEADS,
             N_BRANCH * D_MODEL)
D_IN = sum(IN_SPLITS)

kernel_name = "hybrid_mla_conv_fox_gated_block"


def _split_points():
    return [int(v) for v in np.cumsum(IN_SPLITS)[:-1]]


def rms_norm(x, g):
    xf = x.astype(jnp.float32)
    y = xf * lax.rsqrt(jnp.mean(xf * xf, axis=-1, keepdims=True) + EPS) * g.astype(jnp.float32)
    return y.astype(x.dtype)


def rope_tables(length):
    inv_freq = 1.0 / (ROPE_THETA ** (jnp.arange(0, MLA_ROPE, 2, dtype=jnp.float32) / MLA_ROPE))
    ang = jnp.arange(length, dtype=jnp.float32)[:, None] * inv_freq[None, :]
    return jnp.cos(ang)[:, None, :], jnp.sin(ang)[:, None, :]


def apply_rope(x, cos, sin):
    xf = x.astype(jnp.float32)
    x1, x2 = xf[..., : MLA_ROPE // 2], xf[..., MLA_ROPE // 2:]
    return jnp.concatenate([x1 * cos - x2 * sin, x1 * sin + x2 * cos], axis=-1).astype(x.dtype)


def to_heads(t, n_heads):
    b, l, _ = t.shape
    return t.reshape(b, l, n_heads, -1).transpose(0, 2, 1, 3)


def blocked_causal_attention(q, k, v, scale, decay=None):
    b, h, l, _ = q.shape
    pad = (-l) % BLOCK
    padw = ((0, 0), (0, 0), (pad, 0), (0, 0))
    qp, kp, vp = jnp.pad(q, padw), jnp.pad(k, padw), jnp.pad(v, padw)
    lp = l + pad
    nb = lp // BLOCK
    kpos = jnp.arange(lp)
    key_ok = kpos >= pad
    q_blocks = qp.reshape(b, h, nb, BLOCK, -1).transpose(2, 0, 1, 3, 4)
    dp = None if decay is None else jnp.pad(decay, ((0, 0), (0, 0), (pad, 0)))

    def attend(qi, i, di):
        s = jnp.einsum('bhqd,bhkd->bhqk', qi, kp, preferred_element_type=jnp.float32) * scale
        if di is not None:
            s = s + (di[..., :, None] - dp[:, :, None, :])
        qpos = i * BLOCK + jnp.arange(BLOCK)
        mask = (kpos[None, :] <= qpos[:, None]) & key_ok[None, :]
        p = jax.nn.softmax(jnp.where(mask, s, NEG_INF), axis=-1).astype(vp.dtype)
        return jnp.einsum('bhqk,bhkd->bhqd', p, vp)

    idx = jnp.arange(nb)
    if decay is None:
        out = lax.map(lambda a: attend(a[0], a[1], None), (q_blocks, idx))
    else:
        d_blocks = dp.reshape(b, h, nb, BLOCK).transpose(2, 0, 1, 3)
        out = lax.map(lambda a: attend(a[0], a[1], a[2]), (q_blocks, idx, d_blocks))
    out = out.transpose(1, 2, 0, 3, 4).reshape(b, h, lp, -1)
    return out[:, :, pad:]


def hybrid_mixer(h, w_in, b_forget, g_q_lat, g_kv_lat, w_uq, w_ukv, conv_w, w_branch, w_out, cos, sin):
    b, l, _ = h.shape
    proj = h @ w_in
    (c_q, c_kv, k_pe, conv_b, conv_c, conv_x,
     f_q, f_k, f_v, f_logit, gate_logit) = jnp.split(proj, _split_points(), axis=-1)

    q = (rms_norm(c_q, g_q_lat) @ w_uq).reshape(b, l, MLA_HEADS, MLA_NOPE + MLA_ROPE)
    q_nope, q_pe = q[..., :MLA_NOPE], apply_rope(q[..., MLA_NOPE:], cos, sin)
    kv = (rms_norm(c_kv, g_kv_lat) @ w_ukv).reshape(b, l, MLA_HEADS, MLA_NOPE + MLA_V)
    k_nope, v_a = kv[..., :MLA_NOPE], kv[..., MLA_NOPE:]
    k_pe = apply_rope(k_pe[:, :, None, :], cos, sin)
    q_a = jnp.concatenate([q_nope, q_pe], axis=-1)
    k_a = jnp.concatenate([k_nope, jnp.broadcast_to(k_pe, (b, l, MLA_HEADS, MLA_ROPE))], axis=-1)
    o_a = blocked_causal_attention(q_a.transpose(0, 2, 1, 3), k_a.transpose(0, 2, 1, 3),
                                   v_a.transpose(0, 2, 1, 3), (MLA_NOPE + MLA_ROPE) ** -0.5)
    o_a = o_a.transpose(0, 2, 1, 3).reshape(b, l, MLA_HEADS * MLA_V)

    u = conv_c * conv_x
    u = lax.conv_general_dilated(u, conv_w[:, None, :].astype(u.dtype), window_strides=(1,),
                                 padding=[(CONV_K - 1, 0)], dimension_numbers=('NWC', 'WIO', 'NWC'),
                                 feature_group_count=CONV_WIDTH)
    o_b = conv_b * u

    log_f = jax.nn.log_sigmoid(f_logit.astype(jnp.float32) + b_forget.astype(jnp.float32))
    c = jnp.cumsum(log_f, axis=1).transpose(0, 2, 1)
    o_c = blocked_causal_attention(to_heads(f_q, FOX_HEADS), to_heads(f_k, FOX_HEADS),
                                   to_heads(f_v, FOX_HEADS), FOX_HEAD_DIM ** -0.5, decay=c)
    o_c = o_c.transpose(0, 2, 1, 3).reshape(b, l, FOX_WIDTH)

    o = jnp.stack([o_a, o_b, o_c], axis=2)
    y = jnp.einsum('blnw,nwd->blnd', o, w_branch)
    gates = jax.nn.sigmoid(gate_logit.astype(jnp.float32)).astype(h.dtype).reshape(b, l, N_BRANCH, D_MODEL)
    merged = jnp.sum(gates * y, axis=2)
    return merged @ w_out


def swiglu(h, w_ffn_in, w_ffn_out):
    g, u = jnp.split(h @ w_ffn_in, 2, axis=-1)
    return (jax.nn.silu(g) * u) @ w_ffn_out


def setup_inputs(seed: int = 0) -> dict:
    key = jax.random.key(seed)
    ks = jax.random.split(key, 18)
    f32 = jnp.float32

    def dense(k, shape, fan_in):
        return jax.random.normal(k, shape, f32) * fan_in ** -0.5

    def gain(k, shape):
        return 1.0 + 0.05 * jax.random.normal(k, shape, f32)

    return {
        "x": jax.random.normal(ks[0], (BATCH, SEQ, D_MODEL), f32),
        "meta": jax.random.normal(ks[1], (N_META, D_MODEL), f32),
        "w_in": dense(ks[2], (DEPTH, D_MODEL, D_IN), D_MODEL),
        "b_forget": FORGET_BIAS_MEAN + 0.1 * jax.random.normal(ks[3], (DEPTH, FOX_HEADS), f32),
        "g_q_lat": gain(ks[4], (DEPTH, MLA_Q_RANK)),
        "g_kv_lat": gain(ks[5], (DEPTH, MLA_KV_RANK)),
        "w_uq": dense(ks[6], (DEPTH, MLA_Q_RANK, MLA_HEADS * (MLA_NOPE + MLA_ROPE)), MLA_Q_RANK),
        "w_ukv": dense(ks[7], (DEPTH, MLA_KV_RANK, MLA_HEADS * (MLA_NOPE + MLA_V)), MLA_KV_RANK),
        "conv_w": dense(ks[8], (DEPTH, CONV_K, CONV_WIDTH), CONV_K),
        "w_branch": dense(ks[9], (DEPTH, N_BRANCH, BRANCH_WIDTH, D_MODEL), BRANCH_WIDTH),
        "w_out": dense(ks[10], (DEPTH, D_MODEL, D_MODEL), D_MODEL),
        "w_ffn_in": dense(ks[11], (DEPTH, D_MODEL, 2 * D_FF), D_MODEL),
        "w_ffn_out": dense(ks[12], (DEPTH, D_FF, D_MODEL), D_FF),
        "g_mix_pre": gain(ks[13], (DEPTH, D_MODEL)),
        "g_mix_post": gain(ks[14], (DEPTH, D_MODEL)),
        "g_ffn_pre": gain(ks[15], (DEPTH, D_MODEL)),
        "g_ffn_post": gain(ks[16], (DEPTH, D_MODEL)),
    }


def reference(x, meta, w_in, b_forget, g_q_lat, g_kv_lat, w_uq, w_ukv, conv_w, w_branch, w_out,
              w_ffn_in, w_ffn_out, g_mix_pre, g_mix_post, g_ffn_pre, g_ffn_post):
    b, s, _ = x.shape
    length = N_META + s
    h = jnp.concatenate([jnp.broadcast_to(meta[None].astype(x.dtype), (b, N_META, D_MODEL)), x], axis=1)
    cos, sin = rope_tables(length)
    for layer in range(DEPTH):
        hn = rms_norm(h, g_mix_pre[layer])
        mix = hybrid_mixer(hn, w_in[layer], b_forget[layer], g_q_lat[layer], g_kv_lat[layer],
                           w_uq[layer], w_ukv[layer], conv_w[layer], w_branch[layer], w_out[layer], cos, sin)
        h = h + rms_norm(mix, g_mix_post[layer])
        hn = rms_norm(h, g_ffn_pre[layer])
        h = h + rms_norm(swiglu(hn, w_ffn_in[layer], w_ffn_out[layer]), g_ffn_post[layer])
    return h[:, N_META:]
```

```python
import numpy as np
from contextlib import ExitStack
import concourse.bass as bass
import concourse.mybir as mybir
from concourse.bass_utils import run_bass_kernel_spmd

F32 = mybir.dt.float32
BF16 = mybir.dt.bfloat16
AF = mybir.ActivationFunctionType
ALU = mybir.AluOpType

D = 2048; KC = 16; T = 2064; NMETA = 16; SEQ = 2048; DEPTH = 4
DIN = 13384; DFF = 5632; NFF = 44
EPS = 1e-6
HALF = [(0, 1040), (1040, 1024)]
TILES = [[(0, 16), (16, 512), (528, 512)], [(0, 512), (512, 512)]]
TILE_QB = [[None, 0, 4], [8, 12]]
C_CQ, C_CKV, C_KPE, C_CB, C_CC, C_CX, C_FQ, C_FK, C_FV, C_FL, C_G = 0, 512, 1024, 1088, 2112, 3136, 4160, 5184, 6208, 7232, 7240
G_MIXPRE, G_MIXPOST, G_FFNPRE, G_FFNPOST, G_QLAT, G_KVLAT, G_CONV, G_BF = 0, 16, 32, 48, 64, 68, 72, 96
NGV = 97
ENGS = ("pe", "act", "dve", "pool", "sp")
ND = 16
MAXC = 30000
SAME_SYNC = True
MASKVAL = -30000.0


class Prog:
    def __init__(self):
        self.ops = []; self.lastw = {}; self.readers = {}; self.pending = {}
        self.dma_since = []; self.last_on = {}

    def add(self, eng, fn, reads=(), writes=(), dma=False):
        i = len(self.ops); deps = set()
        if eng in self.pending:
            deps |= self.pending.pop(eng)
        for k in reads:
            w = self.lastw.get(k)
            if w is not None: deps.add(w)
        for k in writes:
            w = self.lastw.get(k)
            if w is not None: deps.add(w)
            deps.update(self.readers.get(k, ()))
        for k in reads: self.readers.setdefault(k, []).append(i)
        for k in writes:
            self.lastw[k] = i; self.readers[k] = []
        fd = []
        for d in deps:
            o = self.ops[d]
            if o[0] == eng and not o[2] and not dma and (eng == "pe" or not SAME_SYNC):
                continue
            fd.append(d); o[4] = True
        self.ops.append([eng, fn, dma, fd, False, None])
        if dma: self.dma_since.append(i)
        else: self.last_on[eng] = i
        return i

    def barrier(self):
        s = set(self.last_on.values()) | set(self.dma_since)
        self.pending = {e: set(s) | self.pending.get(e, set()) for e in ENGS}
        self.dma_since = []; self.lastw = {}; self.readers = {}

    def count(self):
        cnt = {e: 0 for e in ENGS}
        for op in self.ops:
            if not op[2] and op[4] and len(op) <= 6: cnt[op[0]] += 1
        return cnt

    def assign(self, esem, dsem, ccsem=()):
        cnt = {e: 0 for e in ENGS}; dcnt = {e: 0 for e in ENGS}; ncc = 0
        for op in self.ops:
            eng = op[0]
            if len(op) > 6:
                op[5] = (ccsem[ncc % len(ccsem)], ncc // len(ccsem) + 1, 1, None); ncc += 1
            elif op[2]:
                j = dcnt[eng]; dcnt[eng] += 1
                s = dsem[eng][j % ND]
                op[5] = (s, 16 * (j // ND + 1), 16, (s, 16 * (j // ND)) if j >= ND else None)
            elif op[4]:
                c = cnt[eng]; cnt[eng] += 1
                op[5] = (esem[eng][c // MAXC], c % MAXC + 1, 1, None)
        self.dcnt = dcnt

    def run(self, engname, e):
        waited = {}
        for op in self.ops:
            if op[0] != engname: continue
            need = {}
            for d in op[3]:
                s, v = self.ops[d][5][0], self.ops[d][5][1]
                if need.get(id(s), (None, 0))[1] < v: need[id(s)] = (s, v)
            sig = op[5]
            if sig is not None and sig[3] is not None:
                s, v = sig[3]
                if need.get(id(s), (None, 0))[1] < v: need[id(s)] = (s, v)
            for k, (s, v) in need.items():
                if waited.get(k, 0) < v:
                    e.wait_ge(s, v); waited[k] = v
            ins = op[1](e)
            if sig is not None:
                ins.then_inc(sig[0], sig[2])


PAIRS = [[0, 1], [2, 3], [4, 5], [6, 7]]
STOP = 0
TH = 1040
TILES8 = [(0, 16), (16, 512), (528, 512)]
QB8 = [None, 8, 12]
R_M = 1297


def build(nlayers=DEPTH):
    nc = bass.Bass("TRN2", target_bir_lowering=False)
    P = Prog()

    def din(name, shape):
        return nc.dram_tensor(name, list(shape), F32, kind="ExternalInput").ap()

    def dscr(name, shape, dt):
        return nc.dram_tensor(name, list(shape), dt).ap()

    xT = din("xT", [D, 1024]); metaT = din("metaT", [D, NMETA])
    w_in = din("w_in", [DEPTH, D, DIN]); w_uq = din("w_uq", [DEPTH, 512, 1536]); w_ukv = din("w_ukv", [DEPTH, 512, 2048])
    w_branch = din("w_branch", [DEPTH, 3, 1024, D]); w_out = din("w_out", [DEPTH, D, D])
    w_ffn_in = din("w_ffn_in", [DEPTH, D, 2 * DFF]); w_ffn_out = din("w_ffn_out", [DEPTH, DFF, D])
    gv_d = din("gv", [128, DEPTH, NGV]); cos_d = din("cosT", [64, TH]); sin_d = din("sinT", [64, TH])
    mask_d = din("masktri", [128, 128]); ident_d = din("ident", [128, 128]); pm_d = din("pm", [128, 4])
    outT = nc.dram_tensor("outT", [D, 1024], F32, kind="ExternalOutput").ap()
    hT = dscr("hT", [D, TH], F32)
    kS = [dscr("kS%d" % c, [512, TH], BF16) for c in range(4)]; kR = [dscr("kR%d" % c, [1024, TH], BF16) for c in range(4)]
    vS = [dscr("vS%d" % c, [4 * TH, 128], BF16) for c in range(4)]; vR = [dscr("vR%d" % c, [8 * TH, 128], BF16) for c in range(4)]
    peS = dscr("peS", [64, TH], BF16); peR = dscr("peR", [128, TH], BF16)
    mS = dscr("mS", [R_M, 8], F32); mR = dscr("mR", [2 * R_M, 8], F32)

    def sb(name, shape, dt):
        return nc.alloc_sbuf_tensor(name, list(shape), dt)

    YH_raw = sb("YH", [128, 16 * TH], BF16)
    YH = YH_raw[:, :].rearrange("p (c t) -> p c t", c=16)
    AR = sb("AR", [128, 49920], BF16)
    SL = [sb("slab%d" % i, [128, 4096], BF16) for i in range(3)]
    PH = sb("PH", [128, 7168], BF16)
    ATT = sb("ATT", [128, 5280], BF16)
    PT = [sb("pt%d" % i, [128, 512], BF16) for i in range(2)]
    COS = sb("cos", [64, TH], F32); SIN = sb("sin", [64, TH], F32)
    TMP = [sb("tmp%d" % i, [128, 512], F32) for i in range(3)]
    SQ = [sb("sq%d" % i, [128, 512], BF16) for i in range(2)]
    ONES32 = sb("ones32", [128, 512], F32)
    CNEG = sb("cneg", [128, 17, 8], F32)
    GV = sb("gvs", [128, DEPTH, NGV], F32)
    IDB = sb("idb", [128, 128], BF16); IDF = sb("idf", [128, 128], F32)
    MSK = sb("msk", [128, 128], BF16); ONESB = sb("onesb", [128, 128], BF16)
    EPSC = sb("epsc", [128, 1], F32); CARRY = sb("carry", [8, 1], F32); NEGB = sb("negb", [8, 1], F32)
    PM = sb("pmask", [128, 4], F32)
    BT = sb("bt", [128, 8, 2], F32); UM = sb("um", [128, 8, 2], F32); UT = sb("ut", [128, 8, 2], F32); GU = sb("gu", [128, 8, 2], F32)
    DC = sb("dc", [128, 2, 8], F32)
    GC = sb("gc", [8, 1], F32); CT = sb("ct", [128, 9, 8], F32); GCT = sb("gct", [128, 8, 8], F32)
    QR_t = sb("qr", [64, TH], BF16); QR = QR_t[:, :]
    PS = [nc.alloc_psum_tensor("ps%d" % i, [128, 512], F32) for i in range(8)]

    def arv(off, n, dt=BF16):
        a = AR[:, off:off + n]
        return a.bitcast(F32) if dt == F32 else a
    HST = arv(0, 16384, F32).rearrange("p (c t) -> p c t", c=16)
    CQN = arv(0, 4160).rearrange("p (c t) -> p c t", c=4)
    CKN = arv(4160, 4160).rearrange("p (c t) -> p c t", c=4)
    OA = arv(8320, 8320).rearrange("p (c t) -> p c t", c=8)
    OB = arv(16640, 8320).rearrange("p (c t) -> p c t", c=8)
    OC = arv(24960, 8320).rearrange("p (c t) -> p c t", c=8)
    MG = arv(33280, 16640).rearrange("p (c t) -> p c t", c=16)
    CFM = arv(33280, 2080, F32)[0:8, :]
    CRAW = arv(16640, 16640, F32).rearrange("p (c t) -> p c t", c=8)
    ACTB = arv(0, 45760).rearrange("p (c t) -> p c t", c=NFF)

    def wh(i):
        base = i * 2048
        return (PH[:, base:base + 768].rearrange("p (k c) -> p k c", k=4),
                PH[:, base + 768:base + 1024].rearrange("p (k c) -> p k c", k=4),
                PH[:, base + 1024:base + 2048].rearrange("p (k c) -> p k c", k=4))
    KPE = PH[0:64, 4096:4096 + 2064]
    CROW = PH[0:1, 0:2080].bitcast(F32)
    CHI = PH[0:1, 2080:3120]; CLO = PH[0:1, 3120:4160]
    KT = ATT[:, 0:2064]; VV = ATT[:, 2064:2064 + 2176].rearrange("p (b d) -> p b d", b=17); QN = ATT[:, 4240:5280]
    CVB = ATT[:, 0:1040]; CVU = ATT[:, 1040:1040 + 2084].bitcast(F32); CVA = ATT[:, 3124:3124 + 2080].bitcast(F32)

    def dv(pname, fn, reads=(), writes=()):
        return P.add(pname, fn, reads, writes)

    def dma(q, out, in_, reads=(), writes=()):
        return P.add(q, lambda e, o=out, i=in_: e.dma_start(out=o, in_=i), reads, writes, dma=True)

    def gather(snd, rcv, skey, rkey):
        i = P.add("pool", lambda e: e.collective_compute("AllGather", op=ALU.bypass, replica_groups=PAIRS,
                                                         ins=[snd.opt()], outs=[rcv.opt()]), [skey], [rkey])
        P.ops[i].append("cc")

    dma("sp", GV[:, :, :], gv_d, writes=["gv"])
    dma("sp", PM[:, :], pm_d, writes=["const"])
    dma("pool", IDB[:, :], ident_d, writes=["const"])
    dma("pool", MSK[:, :], mask_d, writes=["const"])
    dma("sp", IDF[:, :], ident_d, writes=["const"])
    dma("sp", COS[:, :], cos_d, writes=["rope"])
    dma("sp", SIN[:, :], sin_d, writes=["rope"])
    dv("dve", lambda e: e.memset(ONESB[:, :], 1.0), writes=["const"])
    dv("dve", lambda e: e.memset(ONES32[:, :], 1.0), writes=["const"])
    dv("dve", lambda e: e.memset(EPSC[:, :], EPS), writes=["const"])
    for (t0_, n_, src_) in [(0, NMETA, metaT[:, 0:NMETA])] + [(NMETA + 512 * i, 512, xT[:, 512 * i:512 * i + 512]) for i in range(2)]:
        dma("sp", HST[:, :, 0:n_], src_.rearrange("(c p) t -> p c t", p=128), writes=["hst"])
        dma("sp", hT[:, t0_:t0_ + n_].rearrange("(c p) t -> p c t", p=128), HST[:, :, 0:n_], reads=["hst"], writes=["hT"])
    P.barrier()

    bankctr = [0]; banklist = [[0, 1, 2, 3]]
    def nextbank():
        bl = banklist[0]
        b = bl[bankctr[0] % len(bl)]; bankctr[0] += 1
        return b
    slabctr = [0]

    def load_slab(wd, nk, kp, cols):
        i = slabctr[0] % 3; slabctr[0] += 1
        view = SL[i][0:kp, 0:nk * cols].rearrange("p (k c) -> p k c", k=nk)
        dma("pool", view, wd.rearrange("(k p) c -> p k c", p=kp), writes=[("slab", i)])
        return view, ("slab", i)

    def mm(out, lhsT, rhs, start, stop, reads, writes):
        P.add("pe", lambda e, o=out, l=lhsT, r=rhs, s=start, t=stop: e.matmul(o, l, r, start=s, stop=t), reads, writes)

    def linear(xk, xkeys, nk, kp, wd, ncols, tiles, epi, slabcols=256):
        for c0 in range(0, ncols, slabcols):
            cw = min(slabcols, ncols - c0)
            view, skey = load_slab(wd[:, c0:c0 + cw], nk, kp, cw)
            for mc in range(0, cw, 128):
                mw = min(128, cw - mc)
                for ti, (t0, n) in enumerate(tiles):
                    b = nextbank()
                    for k in range(nk):
                        mm(PS[b][0:mw, 0:n], view[:, k, mc:mc + mw], xk(k, t0, n), k == 0, k == nk - 1,
                           [skey] + xkeys, [("ps", b)])
                    epi(c0 + mc, mw, ti, t0, n, PS[b], ("ps", b))

    def act(out, in_, func, reads, writes, **kw):
        P.add("act", lambda e, o=out, i=in_, f=func, k=kw: e.activation(o, i, f, **k), reads, writes)

    def rstd_from(psb, pskey, n, tmp, tkey, inv_n):
        act(tmp[:, 0:n], psb[:, 0:n], AF.Ln, [pskey, "const"], [tkey], scale=inv_n, bias=EPSC[:, 0:1])
        act(tmp[:, 0:n], tmp[:, 0:n], AF.Exp, [tkey], [tkey], scale=-0.5)

    def kblock(kb):
        return (0, 16) if kb == 0 else (16 + 128 * (kb - 1), 128)

    def layer(l):
        if True:
            g0 = 0; tiles = TILES8; hf = 0
            hn = YH

            def norm_phase(gcol):
                for ti, (t0, n) in enumerate(tiles):
                    dma("sp", HST[:, :, 0:n], hT[:, t0:t0 + n].rearrange("(c p) t -> p c t", p=128), reads=["hT"], writes=["hst"])
                    for c in range(KC):
                        s = SQ[c % 2]
                        act(s[:, 0:n], HST[:, c, 0:n], AF.Square, ["hst"], [("sq", c % 2)])
                        mm(PS[4][:, 0:n], ONESB[:, :], s[:, 0:n], c == 0, c == KC - 1, [("sq", c % 2), "const"], [("ps", 4)])
                    rstd_from(PS[4], ("ps", 4), n, TMP[0], ("tmp", 0), 1.0 / D)
                    for c in range(KC):
                        P.add("dve", lambda e, c=c, t0=t0, n=n: e.scalar_tensor_tensor(
                            hn[:, c, t0:t0 + n], HST[:, c, 0:n], GV[:, l, gcol + c:gcol + c + 1], TMP[0][:, 0:n], ALU.mult, ALU.mult),
                            ["hst", ("tmp", 0), "gv"], ["hn"])

            def residual_phase(gcol, ssbank):
                for ti, (t0, n) in enumerate(tiles):
                    rstd_from(PS[ssbank + ti], ("ps", ssbank + ti), n, TMP[0], ("tmp", 0), 1.0 / D)
                    dma("sp", HST[:, :, 0:n], hT[:, t0:t0 + n].rearrange("(c p) t -> p c t", p=128), reads=["hT"], writes=["hst"])
                    for c in range(KC):
                        tb = TMP[1 + c % 2]; tk = ("tmp", 1 + c % 2)
                        act(tb[:, 0:n], YH[:, c, t0:t0 + n], AF.Copy, ["Y", "gv"], [tk], scale=GV[:, l, gcol + c:gcol + c + 1])
                        P.add("dve", lambda e, n=n, tb=tb: e.tensor_tensor(tb[:, 0:n], tb[:, 0:n], TMP[0][:, 0:n], ALU.mult),
                              [tk, ("tmp", 0)], [tk])
                        P.add("dve", lambda e, c=c, n=n, tb=tb: e.tensor_tensor(HST[:, c, 0:n], HST[:, c, 0:n], tb[:, 0:n], ALU.add),
                              ["hst", tk], ["hst"])
                    last = (l == nlayers - 1)
                    dma("sp", hT[:, t0:t0 + n].rearrange("(c p) t -> p c t", p=128), HST[:, :, 0:n], reads=["hst"], writes=["hT"])
                    if last and ti > 0:
                        o0 = t0 - NMETA
                        dma("sp", outT[:, o0:o0 + n].rearrange("(c p) t -> p c t", p=128), HST[:, :, 0:n], reads=["hst"], writes=["outT"])

            def y_epilogue(ssbank):
                def epi(col, mw, ti, t0, n, ps, pk):
                    c = col // 128
                    P.add("dve", lambda e, c=c, t0=t0, n=n, ps=ps: e.tensor_copy(YH[:, c, t0:t0 + n], ps[:, 0:n]), [pk], [("Y", c, ti)])
                    si = (c + ti) % 2; s = SQ[si]
                    act(s[:, 0:n], YH[:, c, t0:t0 + n], AF.Square, [("Y", c, ti)], [("sq", si)])
                    mm(PS[ssbank + ti][:, 0:n], ONESB[:, :], s[:, 0:n], c == 0, c == KC - 1, [("sq", si), "const"], [("ps", ssbank + ti)])
                return epi

            norm_phase(G_MIXPRE)
            P.barrier()
            hnk = lambda k, t0, n: hn[:, k, t0:t0 + n]

            def epi_lat(col, mw, ti, t0, n, ps, pk):
                c = col // 128
                P.add("dve", lambda e, c=c, t0=t0, n=n, ps=ps: e.tensor_copy(CRAW[:, c, t0:t0 + n], ps[:, 0:n]), [pk], ["craw"])
            linear(hnk, ["hn"], KC, 128, w_in[l][:, 0:1024], 1024, tiles, epi_lat)
            for which, (dst, gcol) in enumerate(((CQN, G_QLAT), (CKN, G_KVLAT))):
                for ti, (t0, n) in enumerate(tiles):
                    for c in range(4):
                        s = SQ[c % 2]
                        act(s[:, 0:n], CRAW[:, which * 4 + c, t0:t0 + n], AF.Square, ["craw"], [("sq", c % 2)])
                        mm(PS[4][:, 0:n], ONESB[:, :], s[:, 0:n], c == 0, c == 3, [("sq", c % 2), "const"], [("ps", 4)])
                    rstd_from(PS[4], ("ps", 4), n, TMP[0], ("tmp", 0), 1.0 / 512)
                    for c in range(4):
                        P.add("dve", lambda e, c=c, t0=t0, n=n, dst=dst, gcol=gcol, which=which: e.scalar_tensor_tensor(
                            dst[:, c, t0:t0 + n], CRAW[:, which * 4 + c, t0:t0 + n], GV[:, l, gcol + c:gcol + c + 1], TMP[0][:, 0:n], ALU.mult, ALU.mult),
                            ["craw", ("tmp", 0), "gv"], ["lat"])

            def rope_evac(psx, kx, psr, kr, t0, n, dst, dkey, scale):
                P.add("dve", lambda e: e.tensor_tensor(TMP[1][0:64, 0:n], psx[0:64, 0:n], COS[:, t0:t0 + n], ALU.mult), [kx, "rope"], [("tmp", 1)])
                P.add("dve", lambda e: e.tensor_tensor(TMP[2][0:64, 0:n], psr[0:64, 0:n], SIN[:, t0:t0 + n], ALU.mult), [kr, "rope"], [("tmp", 2)])
                if scale == 1.0:
                    P.add("dve", lambda e: e.tensor_tensor(dst, TMP[1][0:64, 0:n], TMP[2][0:64, 0:n], ALU.add), [("tmp", 1), ("tmp", 2)], [dkey])
                else:
                    P.add("dve", lambda e: e.tensor_tensor(TMP[1][0:64, 0:n], TMP[1][0:64, 0:n], TMP[2][0:64, 0:n], ALU.add), [("tmp", 1), ("tmp", 2)], [("tmp", 1)])
                    P.add("dve", lambda e: e.tensor_scalar(dst, TMP[1][0:64, 0:n], float(scale), 0.0, ALU.mult, ALU.add), [("tmp", 1)], [dkey])

            def own_col(t0):
                return t0 if t0 < 16 else 1024 + t0

            i0 = slabctr[0] % 3; slabctr[0] += 1
            kpw = SL[i0][:, 0:16 * 128].rearrange("p (k c) -> p k c", k=16)
            wsrc = w_in[l]
            dma("pool", kpw[:, :, 0:64], wsrc[:, C_KPE:C_KPE + 64].rearrange("(k p) c -> p k c", p=128), writes=[("slab", i0)])
            dma("pool", kpw[:, :, 64:96], wsrc[:, C_KPE + 32:C_KPE + 64].rearrange("(k p) c -> p k c", p=128), writes=[("slab", i0)])
            dma("pool", kpw[:, :, 96:128], wsrc[:, C_KPE:C_KPE + 32].rearrange("(k p) c -> p k c", p=128), writes=[("slab", i0)])
            for ti, (t0, n) in enumerate(tiles):
                b1 = nextbank(); b2 = nextbank()
                for k in range(KC):
                    mm(PS[b1][0:64, 0:n], kpw[:, k, 0:64], hnk(k, t0, n), k == 0, k == KC - 1, [("slab", i0), "hn"], [("ps", b1)])
                for k in range(KC):
                    mm(PS[b2][0:64, 0:n], kpw[:, k, 64:128], hnk(k, t0, n), k == 0, k == KC - 1, [("slab", i0), "hn"], [("ps", b2)])
                oc_ = own_col(t0)
                rope_evac(PS[b1], ("ps", b1), PS[b2], ("ps", b2), t0, n, KPE[:, oc_:oc_ + n], "kpe", 1.0)
            dma("sp", peS[:, 0:16], KPE[:, 0:16], reads=["kpe"], writes=["peS"])
            dma("sp", peS[:, 16:TH], KPE[:, 1040:2064], reads=["kpe"], writes=["peS"])

            def store_kv(a, h, ktile, ktkey):
                u = a * 8 + h; c = u // 4; r0 = (u % 4) * 128; v0 = (u % 4) * TH
                dma("sp", kS[c][r0:r0 + 128, :], ktile, reads=[ktkey], writes=[("kS", c)])
                dma("sp", vS[c][v0:v0 + 16, :], VV[0:16, 0, :], reads=["v"], writes=[("vS", c)])
                dma("sp", vS[c][v0 + 16:v0 + TH, :].rearrange("(b p) d -> p b d", p=128), VV[:, 1:9, :], reads=["v"], writes=[("vS", c)])

            for h in range(8):
                wq, wqs, wkv = wh(h % 2); wkey = ("wh", h % 2)
                dma("pool", wkv, w_ukv[l][:, h * 256:(h + 1) * 256].rearrange("(k p) c -> p k c", p=128), writes=[wkey])
                kst = KT[:, 0:TH] if h % 2 == 0 else QN[:, 0:TH]; kstk = ("kst", h % 2)
                for ti, (t0, n) in enumerate(tiles):
                    b = nextbank()
                    for k in range(4):
                        mm(PS[b][:, 0:n], wkv[:, k, 0:128], CKN[:, k, t0:t0 + n], k == 0, k == 3, [wkey, "lat"], [("ps", b)])
                    act(kst[:, t0:t0 + n], PS[b][:, 0:n], AF.Copy, [("ps", b)], [kstk])
                for kb in range(9):
                    k0, kn = kblock(kb)
                    b = nextbank()
                    for k in range(4):
                        mm(PS[b][0:kn, 0:128], CKN[:, k, k0:k0 + kn], wkv[:, k, 128:256], k == 0, k == 3, [wkey, "lat"], [("ps", b)])
                    act(VV[0:kn, kb, :], PS[b][0:kn, 0:128], AF.Copy, [("ps", b)], ["v"])
                store_kv(0, h, kst, kstk)

            for c in (0, 1):
                gather(kS[c], kR[c], ("kS", c), ("kR", c)); gather(vS[c], vR[c], ("vS", c), ("vR", c))
            gather(peS, peR, "peS", "peR")

            dv("dve", lambda e: e.memset(CARRY[:, :], 0.0), writes=["carry"])
            dv("dve", lambda e: e.tensor_scalar(NEGB[:, :], GV[0:8, l, G_BF:G_BF + 1], -1.0, 0.0, ALU.mult, ALU.add), reads=["gv"], writes=["negb"])
            view, skey = load_slab(w_in[l][:, C_FL:C_FL + 8], KC, 128, 8)
            for ti, (t0, n) in enumerate(tiles):
                b = nextbank()
                for k in range(KC):
                    mm(PS[b][0:8, 0:n], view[:, k, 0:8], hnk(k, t0, n), k == 0, k == KC - 1, [skey, "hn"], [("ps", b)])
                act(TMP[1][0:8, 0:n], PS[b][0:8, 0:n], AF.Exp, [("ps", b), "negb"], [("tmp", 1)], scale=-1.0, bias=NEGB[:, 0:1])
                act(TMP[1][0:8, 0:n], TMP[1][0:8, 0:n], AF.Ln, [("tmp", 1)], [("tmp", 1)], bias=1.0)
                P.add("dve", lambda e, t0=t0, n=n: e.tensor_tensor_scan(CFM[:, t0:t0 + n], ONES32[0:8, 0:n], TMP[1][0:8, 0:n], CARRY[:, 0:1], ALU.mult, ALU.subtract),
                      [("tmp", 1), "carry", "const"], ["cfm"])
                P.add("dve", lambda e, t0=t0, n=n: e.tensor_copy(CARRY[:, 0:1], CFM[:, t0 + n - 1:t0 + n]), ["cfm"], ["carry"])
            for kb in range(9):
                k0, kn = kblock(kb)
                b = nextbank()
                P.add("pe", lambda e, b=b, k0=k0, kn=kn: e.transpose(PS[b][0:kn, 0:8], CFM[:, k0:k0 + kn], IDF[0:8, 0:8]), ["cfm", "const"], [("ps", b)])
                P.add("dve", lambda e, b=b, kb=kb, kn=kn: e.tensor_copy(CT[0:kn, kb, :], PS[b][0:kn, 0:8]), [("ps", b)], ["ct"])
            dma("sp", mS[0:16, :], CT[0:16, 0, :], reads=["ct"], writes=["mS"])
            dma("sp", mS[16:TH, :].rearrange("(b p) h -> p b h", p=128), CT[:, 1:9, :], reads=["ct"], writes=["mS"])
            dma("sp", mS[1296:1297, :].rearrange("a h -> h a"), CFM[:, TH - 1:TH], reads=["cfm"], writes=["mS"])

            for h in range(8):
                vk, kk = load_slab(w_in[l][:, C_FK + 128 * h:C_FK + 128 * h + 128], KC, 128, 128)
                kst = KT[:, 0:TH] if h % 2 == 0 else QN[:, 0:TH]; kstk = ("kst", h % 2)
                for ti, (t0, n) in enumerate(tiles):
                    b = nextbank()
                    for k in range(KC):
                        mm(PS[b][:, 0:n], vk[:, k, :], hnk(k, t0, n), k == 0, k == KC - 1, [kk, "hn"], [("ps", b)])
                    act(kst[:, t0:t0 + n], PS[b][:, 0:n], AF.Copy, [("ps", b)], [kstk])
                vv, kvk = load_slab(w_in[l][:, C_FV + 128 * h:C_FV + 128 * h + 128], KC, 128, 128)
                for kb in range(9):
                    k0, kn = kblock(kb)
                    b = nextbank()
                    for k in range(KC):
                        mm(PS[b][0:kn, 0:128], hn[:, k, k0:k0 + kn], vv[:, k, :], k == 0, k == KC - 1, [kvk, "hn"], [("ps", b)])
                    act(VV[0:kn, kb, :], PS[b][0:kn, 0:128], AF.Copy, [("ps", b)], ["v"])
                store_kv(1, h, kst, kstk)
            P.barrier()
            for c in (2, 3):
                gather(kS[c], kR[c], ("kS", c), ("kR", c)); gather(vS[c], vR[c], ("vS", c), ("vR", c))

            for j in range(8):
                vb, kb_ = load_slab(w_in[l][:, C_CB + 128 * j:C_CB + 128 * j + 128], KC, 128, 128)
                vc, kc_ = load_slab(w_in[l][:, C_CC + 128 * j:C_CC + 128 * j + 128], KC, 128, 128)
                vx, kx_ = load_slab(w_in[l][:, C_CX + 128 * j:C_CX + 128 * j + 128], KC, 128, 128)
                dv("dve", lambda e: e.memset(CVU[:, 0:2], 0.0), writes=["cvu"])
                for ti, (t0, n) in enumerate(tiles):
                    b = nextbank()
                    for k in range(KC):
                        mm(PS[b][:, 0:n], vb[:, k, :], hnk(k, t0, n), k == 0, k == KC - 1, [kb_, "hn"], [("ps", b)])
                    act(CVB[:, t0:t0 + n], PS[b][:, 0:n], AF.Copy, [("ps", b)], ["cvb"])
                    if ti == 1:
                        P.add("dve", lambda e, b=b, j=j: e.tensor_copy(BT[:, j, :], PS[b][:, 0:2]), [("ps", b)], ["bt"])
                    b1 = nextbank()
                    for k in range(KC):
                        mm(PS[b1][:, 0:n], vc[:, k, :], hnk(k, t0, n), k == 0, k == KC - 1, [kc_, "hn"], [("ps", b1)])
                    act(TMP[1][:, 0:n], PS[b1][:, 0:n], AF.Copy, [("ps", b1)], [("tmp", 1)])
                    b2 = nextbank()
                    for k in range(KC):
                        mm(PS[b2][:, 0:n], vx[:, k, :], hnk(k, t0, n), k == 0, k == KC - 1, [kx_, "hn"], [("ps", b2)])
                    P.add("dve", lambda e, b2=b2, t0=t0, n=n: e.tensor_tensor(CVU[:, 2 + t0:2 + t0 + n], PS[b2][:, 0:n], TMP[1][:, 0:n], ALU.mult),
                          [("ps", b2), ("tmp", 1)], ["cvu"])
                gc = G_CONV + 3 * j
                dv("dve", lambda e, gc=gc: e.tensor_scalar(CVA[:, 0:TH], CVU[:, 2:2 + TH], GV[:, l, gc + 2:gc + 3], 0.0, ALU.mult, ALU.add), reads=["cvu", "gv"], writes=["cva"])
                dv("dve", lambda e, gc=gc: e.scalar_tensor_tensor(CVA[:, 0:TH], CVU[:, 1:1 + TH], GV[:, l, gc + 1:gc + 2], CVA[:, 0:TH], ALU.mult, ALU.add), reads=["cvu", "cva", "gv"], writes=["cva"])
                dv("dve", lambda e, gc=gc: e.scalar_tensor_tensor(CVA[:, 0:TH], CVU[:, 0:TH], GV[:, l, gc:gc + 1], CVA[:, 0:TH], ALU.mult, ALU.add), reads=["cvu", "cva", "gv"], writes=["cva"])
                dv("dve", lambda e, j=j: e.tensor_tensor(OB[:, j, 0:TH], CVA[:, 0:TH], CVB[:, 0:TH], ALU.mult), reads=["cva", "cvb"], writes=["ob"])
                dv("dve", lambda e, j=j: e.tensor_copy(UM[:, j, :], CVU[:, 16:18]), reads=["cvu"], writes=["um"])
                dv("dve", lambda e, j=j: e.tensor_copy(UT[:, j, :], CVU[:, TH:TH + 2]), reads=["cvu"], writes=["ut"])
            dma("sp", mS[1040:1296, :].rearrange("(p a) b -> p (a b)", a=2), UT[:, :, :].rearrange("p j k -> p (j k)"), reads=["ut"], writes=["mS"])

            if STOP == 1: return
            gather(mS, mR, "mS", "mR")
            P.barrier()

            if STOP == 2: return
            dma("sp", KPE[:, 16:1040], peR[0:64, 16:TH], reads=["peR"], writes=["kpe"])
            dma("sp", GC[:, :], mR[1296:1297, :].rearrange("a h -> h a"), reads=["mR"], writes=["gc"])
            dma("sp", GCT[:, :, :], mR[16:TH, :].rearrange("(b p) h -> p b h", p=128), reads=["mR"], writes=["gct"])
            dma("sp", GU[:, :, :].rearrange("p j k -> p (j k)"), mR[1040:1296, :].rearrange("(p a) b -> p (a b)", a=2), reads=["mR"], writes=["gu"])
            dv("dve", lambda e: e.tensor_tensor(GC[:, :], GC[:, :], CFM[:, 15:16], ALU.subtract), reads=["gc", "cfm"], writes=["gc"])
            dv("dve", lambda e: e.tensor_tensor(GC[:, :], GC[:, :], PM[0:8, 1:2], ALU.mult), reads=["gc", "const"], writes=["gc"])
            dv("dve", lambda e: e.tensor_scalar(CFM[:, 16:TH], CFM[:, 16:TH], GC[:, 0:1], 0.0, ALU.add, ALU.add), reads=["gc", "cfm"], writes=["cfm"])
            for kb in range(9):
                k0, kn = kblock(kb)
                b = nextbank()
                P.add("pe", lambda e, b=b, k0=k0, kn=kn: e.transpose(PS[b][0:kn, 0:8], CFM[:, k0:k0 + kn], IDF[0:8, 0:8]), ["cfm", "const"], [("ps", b)])
                dst = 0 if kb == 0 else kb + 8
                P.add("dve", lambda e, b=b, dst=dst, kn=kn: e.tensor_scalar(CNEG[0:kn, dst, :], PS[b][0:kn, 0:8], -1.0, 0.0, ALU.mult, ALU.add), [("ps", b)], ["cneg"])
            dv("dve", lambda e: e.tensor_scalar(CNEG[:, 1:9, :], GCT[:, :, :], -1.0, PM[:, 0:1], ALU.mult, ALU.add), reads=["gct", "const"], writes=["cneg"])
            dv("dve", lambda e: e.tensor_tensor(GU[:, :, :], GU[:, :, :], UM[:, :, :], ALU.subtract), reads=["gu", "um"], writes=["gu"])
            dv("dve", lambda e: e.tensor_scalar(GU[:, :, :], GU[:, :, :], PM[:, 1:2], 0.0, ALU.mult, ALU.add), reads=["gu", "const"], writes=["gu"])
            CW = GV[:, l, G_CONV:G_CONV + 24].rearrange("p (j k) -> p j k", k=3)
            dv("dve", lambda e: e.tensor_tensor(DC[:, 0, :], CW[:, :, 1], GU[:, :, 1], ALU.mult), reads=["gu", "gv"], writes=["dc"])
            dv("dve", lambda e: e.tensor_tensor(DC[:, 1, :], CW[:, :, 0], GU[:, :, 0], ALU.mult), reads=["gu", "gv"], writes=["dc"])
            dv("dve", lambda e: e.tensor_tensor(DC[:, 0, :], DC[:, 0, :], DC[:, 1, :], ALU.add), reads=["dc"], writes=["dc"])
            dv("dve", lambda e: e.tensor_tensor(DC[:, 1, :], CW[:, :, 0], GU[:, :, 1], ALU.mult), reads=["gu", "gv", "dc"], writes=["dc"])
            dv("dve", lambda e: e.tensor_tensor(DC[:, 0, :], DC[:, 0, :], BT[:, :, 0], ALU.mult), reads=["dc", "bt"], writes=["dc"])
            dv("dve", lambda e: e.tensor_tensor(DC[:, 1, :], DC[:, 1, :], BT[:, :, 1], ALU.mult), reads=["dc", "bt"], writes=["dc"])
            dv("dve", lambda e: e.tensor_tensor(OB[:, :, 16], OB[:, :, 16], DC[:, 0, :], ALU.add), reads=["dc", "ob"], writes=["ob"])
            dv("dve", lambda e: e.tensor_tensor(OB[:, :, 17], OB[:, :, 17], DC[:, 1, :], ALU.add), reads=["dc", "ob"], writes=["ob"])

            if STOP == 3: return
            def load_kv(a, h):
                u = a * 8 + h; c = u // 4; r0 = (u % 4) * 128; v0 = (u % 4) * TH
                dma("sp", KT[:, 0:16], kS[c][r0:r0 + 128, 0:16], reads=[("kS", c)], writes=["kt"])
                dma("sp", KT[:, 16:1040], kR[c][r0:r0 + 128, 16:TH], reads=[("kR", c)], writes=["kt"])
                dma("sp", KT[:, 1040:2064], kS[c][r0:r0 + 128, 16:TH], reads=[("kS", c)], writes=["kt"])
                dma("sp", VV[0:16, 0, :], vS[c][v0:v0 + 16, :], reads=[("vS", c)], writes=["v"])
                dma("sp", VV[:, 1:9, :], vR[c][v0 + 16:v0 + TH, :].rearrange("(b p) d -> p b d", p=128), reads=[("vR", c)], writes=["v"])
                dma("sp", VV[:, 9:17, :], vS[c][v0 + 16:v0 + TH, :].rearrange("(b p) d -> p b d", p=128), reads=[("vS", c)], writes=["v"])

            def attention(h, Oview, okey, qparts, kparts, fox):
                for ti, (t0, n) in enumerate(tiles):
                    qb0 = QB8[ti]
                    kbs = [0] if qb0 is None else [kb for kb in range(17) if kb == 0 or (kb - 1) <= qb0 + 3]
                    for ii, kb in enumerate(kbs):
                        k0, kn = kblock(kb)
                        diag = False
                        if qb0 is None:
                            c0 = 0; diag = True
                        elif kb == 0 or (kb - 1) < qb0:
                            c0 = 0
                        else:
                            c0 = (kb - 1 - qb0) * 128; diag = True
                        nn = n - c0
                        sbk = 4 + (ii % 2); spk = ("ps", sbk); S = PS[sbk]
                        nparts = len(qparts) + (2 if fox else 0) + (1 if diag else 0)
                        j = 0
                        for (qf, kf) in zip(qparts, kparts):
                            mm(S[0:kn, 0:nn], kf(k0, kn), qf(t0 + c0, nn), j == 0, j == nparts - 1, ["kt", "q", "kpe"], [spk]); j += 1
                        if fox:
                            mm(S[0:kn, 0:nn], ONESB[0:1, 0:kn], CHI[:, t0 + c0:t0 + c0 + nn], False, j == nparts - 1, ["crow", "const"], [spk]); j += 1
                            mm(S[0:kn, 0:nn], ONESB[0:1, 0:kn], CLO[:, t0 + c0:t0 + c0 + nn], False, j == nparts - 1, ["crow", "const"], [spk]); j += 1
                        if diag:
                            dn = min(kn, nn)
                            mm(S[0:kn, 0:dn], IDB[0:kn, 0:kn], MSK[0:kn, 0:dn], False, True, ["const"], [spk]); j += 1
                        pt = PT[ii % 2]; ptk = ("pt", ii % 2)
                        partner = 1 <= kb <= 8
                        if fox:
                            act(pt[0:kn, 0:nn], S[0:kn, 0:nn], AF.Exp, [spk, "cneg"], [ptk], bias=CNEG[0:kn, kb, h:h + 1])
                        elif partner:
                            act(pt[0:kn, 0:nn], S[0:kn, 0:nn], AF.Exp, [spk, "const"], [ptk], bias=PM[0:kn, 0:1])
                        else:
                            act(pt[0:kn, 0:nn], S[0:kn, 0:nn], AF.Exp, [spk], [ptk])
                        first = ii == 0; lastk = ii == len(kbs) - 1
                        mm(PS[6][:, c0:n], VV[0:kn, kb, :], pt[0:kn, 0:nn], first, lastk, [ptk, "v"], [("ps", 6)])
                        mm(PS[7][:, c0:n], ONESB[0:kn, :], pt[0:kn, 0:nn], first, lastk, [ptk, "const"], [("ps", 7)])
                    P.add("dve", lambda e, n=n: e.reciprocal(TMP[0][:, 0:n], PS[7][:, 0:n]), [("ps", 7)], [("tmp", 0)])
                    P.add("dve", lambda e, t0=t0, n=n: e.tensor_tensor(Oview[:, h, t0:t0 + n], PS[6][:, 0:n], TMP[0][:, 0:n], ALU.mult),
                          [("ps", 6), ("tmp", 0)], [okey])

            SC_A = 192.0 ** -0.5
            for h in range(8):
                wq, wqs, wkv = wh(h % 2); wkey = ("wh", h % 2)
                dma("pool", wq, w_uq[l][:, h * 192:(h + 1) * 192].rearrange("(k p) c -> p k c", p=128), writes=[wkey])
                dma("pool", wqs[:, :, 0:32], w_uq[l][:, h * 192 + 160:h * 192 + 192].rearrange("(k p) c -> p k c", p=128), writes=[wkey])
                dma("pool", wqs[:, :, 32:64], w_uq[l][:, h * 192 + 128:h * 192 + 160].rearrange("(k p) c -> p k c", p=128), writes=[wkey])
                load_kv(0, h)
                for ti, (t0, n) in enumerate(tiles):
                    b = nextbank()
                    for k in range(4):
                        mm(PS[b][:, 0:n], wq[:, k, 0:128], CQN[:, k, t0:t0 + n], k == 0, k == 3, [wkey, "lat"], [("ps", b)])
                    P.add("dve", lambda e, b=b, t0=t0, n=n: e.tensor_scalar(QN[:, t0:t0 + n], PS[b][:, 0:n], SC_A, 0.0, ALU.mult, ALU.add), [("ps", b)], ["q"])
                    b1 = nextbank(); b2 = nextbank()
                    for k in range(4):
                        mm(PS[b1][0:64, 0:n], wq[:, k, 128:192], CQN[:, k, t0:t0 + n], k == 0, k == 3, [wkey, "lat"], [("ps", b1)])
                    for k in range(4):
                        mm(PS[b2][0:64, 0:n], wqs[:, k, :], CQN[:, k, t0:t0 + n], k == 0, k == 3, [wkey, "lat"], [("ps", b2)])
                    rope_evac(PS[b1], ("ps", b1), PS[b2], ("ps", b2), t0, n, QR[:, t0:t0 + n], "q", SC_A)
                attention(h, OA, "oa",
                          [lambda t, n: QN[:, t:t + n], lambda t, n: QR[:, t:t + n]],
                          [lambda k0, kn: KT[:, k0:k0 + kn], lambda k0, kn: KPE[:, k0:k0 + kn]], False)
            P.barrier()
            if STOP == 4: return

            SC_C = 128.0 ** -0.5
            for h in range(8):
                dma("sp", CROW[:, 0:TH], CFM[h:h + 1, 0:TH], reads=["cfm"], writes=["crow32"])
                dv("dve", lambda e: e.tensor_copy(CHI[:, 0:TH], CROW[:, 0:TH]), reads=["crow32"], writes=["crow"])
                dv("dve", lambda e: e.tensor_tensor(CROW[:, 0:TH], CROW[:, 0:TH], CHI[:, 0:TH], ALU.subtract), reads=["crow32", "crow"], writes=["crow32"])
                dv("dve", lambda e: e.tensor_copy(CLO[:, 0:TH], CROW[:, 0:TH]), reads=["crow32"], writes=["crow"])
                load_kv(1, h)
                vq, kq = load_slab(w_in[l][:, C_FQ + 128 * h:C_FQ + 128 * h + 128], KC, 128, 128)
                for ti, (t0, n) in enumerate(tiles):
                    b = nextbank()
                    for k in range(KC):
                        mm(PS[b][:, 0:n], vq[:, k, :], hnk(k, t0, n), k == 0, k == KC - 1, [kq, "hn"], [("ps", b)])
                    P.add("dve", lambda e, b=b, t0=t0, n=n: e.tensor_scalar(QN[:, t0:t0 + n], PS[b][:, 0:n], SC_C, 0.0, ALU.mult, ALU.add), [("ps", b)], ["q"])
                attention(h, OC, "oc", [lambda t, n: QN[:, t:t + n]], [lambda k0, kn: KT[:, k0:k0 + kn]], True)

            P.barrier()
            ACC = [ATT[:, i * 1024:(i + 1) * 1024].bitcast(F32) for i in range(3)]
            OBR = (OA, OB, OC); OKEY = ("oa", "ob", "oc")
            for m in range(16):
                for nb in range(3):
                    bview, bkey = load_slab(w_branch[l, nb][:, m * 128:m * 128 + 128], 8, 128, 128)
                    gview, gkey = load_slab(w_in[l][:, C_G + nb * D + m * 128:C_G + nb * D + m * 128 + 128], KC, 128, 128)
                    bys = []
                    for ti, (t0, n) in enumerate(tiles):
                        by = nextbank(); bys.append(by)
                        for k in range(8):
                            mm(PS[by][:, 0:n], bview[:, k, :], OBR[nb][:, k, t0:t0 + n], k == 0, k == 7, [bkey, OKEY[nb]], [("ps", by)])
                    for ti, (t0, n) in enumerate(tiles):
                        by = bys[ti]
                        bg = nextbank()
                        for k in range(KC):
                            mm(PS[bg][:, 0:n], gview[:, k, :], hnk(k, t0, n), k == 0, k == KC - 1, [gkey, "hn"], [("ps", bg)])
                        act(TMP[1][:, 0:n], PS[bg][:, 0:n], AF.Sigmoid, [("ps", bg)], [("tmp", 1)])
                        acc = ACC[ti]; ak = ("acc", ti)
                        if nb == 0:
                            P.add("dve", lambda e, by=by, n=n, acc=acc: e.tensor_tensor(acc[:, 0:n], PS[by][:, 0:n], TMP[1][:, 0:n], ALU.mult),
                                  [("ps", by), ("tmp", 1)], [ak])
                        else:
                            P.add("dve", lambda e, by=by, n=n: e.tensor_tensor(TMP[2][:, 0:n], PS[by][:, 0:n], TMP[1][:, 0:n], ALU.mult),
                                  [("ps", by), ("tmp", 1)], [("tmp", 2)])
                            if nb == 1:
                                P.add("dve", lambda e, n=n, acc=acc: e.tensor_tensor(acc[:, 0:n], acc[:, 0:n], TMP[2][:, 0:n], ALU.add),
                                      [ak, ("tmp", 2)], [ak])
                            else:
                                P.add("dve", lambda e, m=m, t0=t0, n=n, acc=acc: e.tensor_tensor(MG[:, m, t0:t0 + n], acc[:, 0:n], TMP[2][:, 0:n], ALU.add),
                                      [ak, ("tmp", 2)], ["mg"])
            P.barrier()

            linear(lambda k, t0, n: MG[:, k, t0:t0 + n], ["mg"], KC, 128, w_out[l], D, tiles, y_epilogue(4))
            P.barrier()
            residual_phase(G_MIXPOST, 4)
            P.barrier()

            norm_phase(G_FFNPRE)
            P.barrier()
            for j in range(NFF):
                gsl, gk = load_slab(w_ffn_in[l][:, j * 128:j * 128 + 128], KC, 128, 128)
                usl, uk = load_slab(w_ffn_in[l][:, DFF + j * 128:DFF + j * 128 + 128], KC, 128, 128)
                for ti, (t0, n) in enumerate(tiles):
                    bg = nextbank()
                    for k in range(KC):
                        mm(PS[bg][:, 0:n], gsl[:, k, :], hnk(k, t0, n), k == 0, k == KC - 1, [gk, "hn"], [("ps", bg)])
                    act(TMP[ti][:, 0:n], PS[bg][:, 0:n], AF.Silu, [("ps", bg)], [("tmp", ti)])
                for ti, (t0, n) in enumerate(tiles):
                    bu = nextbank()
                    for k in range(KC):
                        mm(PS[bu][:, 0:n], usl[:, k, :], hnk(k, t0, n), k == 0, k == KC - 1, [uk, "hn"], [("ps", bu)])
                    P.add("dve", lambda e, j=j, bu=bu, t0=t0, n=n, ti=ti: e.tensor_tensor(ACTB[:, j, t0:t0 + n], PS[bu][:, 0:n], TMP[ti][:, 0:n], ALU.mult),
                          [("ps", bu), ("tmp", ti)], ["actb"])
            P.barrier()
            yep = y_epilogue(4)
            for m in range(16):
                s1, k1 = load_slab(w_ffn_out[l][0:2816, m * 128:m * 128 + 128], 22, 128, 128)
                s2, k2 = load_slab(w_ffn_out[l][2816:5632, m * 128:m * 128 + 128], 22, 128, 128)
                for ti, (t0, n) in enumerate(tiles):
                    b = nextbank()
                    for k in range(NFF):
                        sv, sk = (s1, k1) if k < 22 else (s2, k2)
                        mm(PS[b][:, 0:n], sv[:, k % 22, :], ACTB[:, k, t0:t0 + n], k == 0, k == NFF - 1, [sk, "actb"], [("ps", b)])
                    yep(m * 128, 128, ti, t0, n, PS[b], ("ps", b))
            P.barrier()
            residual_phase(G_FFNPOST, 4)
            P.barrier()


    for l_ in range(nlayers):
        layer(l_)

    cnt = P.count()
    ncc = sum(1 for op in P.ops if len(op) > 6)
    with ExitStack() as es:
        esem = {e: [es.enter_context(nc.semaphore("s_%s_%d" % (e, i))) for i in range(cnt[e] // MAXC + 1)] for e in ENGS}
        dsem = {e: [es.enter_context(nc.semaphore("d_%s_%d" % (e, i))) for i in range(ND)] for e in ("sp", "pool")}
        ccsem = [es.enter_context(nc.semaphore("cc_%d" % i)) for i in range(min(ncc, 10))]
        for e in ("pe", "act", "dve"): dsem[e] = []
        P.assign(esem, dsem, ccsem)
        block = es.enter_context(nc.Block())

        @block.tensor
        def _(e): P.run("pe", e)

        @block.scalar
        def _(e): P.run("act", e)

        @block.vector
        def _(e): P.run("dve", e)

        @block.gpsimd
        def _(e): P.run("pool", e)

        @block.sync
        def _(e):
            P.run("sp", e)
            for q in ("sp", "pool"):
                tot = P.dcnt[q]
                for i in range(min(ND, tot)):
                    nfin = (tot - i + ND - 1) // ND
                    e.wait_ge(dsem[q][i], 16 * nfin)
    return nc


def _host_consts():
    inv_freq = (1.0 / (10000.0 ** (np.arange(0, 64, 2, dtype=np.float32) / np.float32(64)))).astype(np.float32)
    ang = np.arange(T, dtype=np.float32)[:, None] * inv_freq[None, :]
    c = np.cos(ang).astype(np.float32).T; s = np.sin(ang).astype(np.float32).T
    cosT = np.concatenate([c, c], 0); sinT = np.concatenate([-s, s], 0)
    k = np.arange(128)[:, None]; q = np.arange(128)[None, :]
    masktri = np.where(k > q, MASKVAL, 0.0).astype(np.float32)
    ident = np.eye(128, dtype=np.float32)
    return cosT, sinT, masktri, ident


_NC_CACHE = {}


def kernel(x, meta, w_in, b_forget, g_q_lat, g_kv_lat, w_uq, w_ukv, conv_w, w_branch, w_out,
           w_ffn_in, w_ffn_out, g_mix_pre, g_mix_post, g_ffn_pre, g_ffn_post, _nlayers=DEPTH, _cores=None):
    f = lambda a: np.ascontiguousarray(np.asarray(a, dtype=np.float32))
    x = f(x); meta = f(meta)
    gv = np.zeros((128, DEPTH, NGV), np.float32)
    def fm(a, nch):
        return np.transpose(np.asarray(a, np.float32).reshape(DEPTH, nch, 128), (2, 0, 1))
    gv[:, :, G_MIXPRE:G_MIXPRE + 16] = fm(g_mix_pre, 16); gv[:, :, G_MIXPOST:G_MIXPOST + 16] = fm(g_mix_post, 16)
    gv[:, :, G_FFNPRE:G_FFNPRE + 16] = fm(g_ffn_pre, 16); gv[:, :, G_FFNPOST:G_FFNPOST + 16] = fm(g_ffn_post, 16)
    gv[:, :, G_QLAT:G_QLAT + 4] = fm(g_q_lat, 4); gv[:, :, G_KVLAT:G_KVLAT + 4] = fm(g_kv_lat, 4)
    cw = np.asarray(conv_w, np.float32).reshape(DEPTH, 3, 8, 128)
    gv[:, :, G_CONV:G_CONV + 24] = np.transpose(cw, (3, 0, 2, 1)).reshape(128, DEPTH, 24)
    gv[0:8, :, G_BF] = np.asarray(b_forget, np.float32).T
    cosT, sinT, masktri, ident = _host_consts()
    global PAIRS
    cores = list(range(8)) if _cores is None else _cores
    PAIRS = [[2 * i, 2 * i + 1] for i in range(len(cores) // 2)]
    ck = (_nlayers, len(cores))
    if ck not in _NC_CACHE:
        _NC_CACHE[ck] = build(_nlayers)
    nc = _NC_CACHE[ck]
    common = dict(metaT=np.ascontiguousarray(meta.T), w_in=f(w_in), w_uq=f(w_uq), w_ukv=f(w_ukv), w_branch=f(w_branch),
                  w_out=f(w_out), w_ffn_in=f(w_ffn_in), w_ffn_out=f(w_ffn_out), gv=gv, masktri=masktri, ident=ident)
    in_maps = []
    for c in range(8):
        b, s = c // 2, c % 2
        m = dict(common)
        m["xT"] = np.ascontiguousarray(x[b, s * 1024:(s + 1) * 1024].T)
        pos = np.concatenate([np.arange(16), 16 + s * 1024 + np.arange(1024)])
        m["cosT"] = np.ascontiguousarray(cosT[:, pos]); m["sinT"] = np.ascontiguousarray(sinT[:, pos])
        pm = np.zeros((128, 4), np.float32)
        pm[:, 0] = 0.0 if s == 1 else MASKVAL
        pm[:, 1] = float(s)
        m["pm"] = pm
        in_maps.append(m)
    res = run_bass_kernel_spmd(nc, [in_maps[c] for c in cores], core_ids=list(range(len(cores))))
    out = np.zeros((4, SEQ, D), np.float32)
    for i, c in enumerate(cores):
        b, s = c // 2, c % 2
        out[b, s * 1024:(s + 1) * 1024] = np.asarray(res.results[i]["outT"]).T
    return out
```

```python
import numpy as np
from contextlib import ExitStack
import concourse.bass as bass
import concourse.mybir as mybir
from concourse.bass_utils import run_bass_kernel_spmd

F32 = mybir.dt.float32
BF16 = mybir.dt.bfloat16
AF = mybir.ActivationFunctionType
ALU = mybir.AluOpType

D = 2048; KC = 16; T = 2064; NMETA = 16; SEQ = 2048; DEPTH = 4
DIN = 13384; DFF = 5632; NFF = 44
EPS = 1e-6
HALF = [(0, 1040), (1040, 1024)]
TILES = [[(0, 16), (16, 512), (528, 512)], [(0, 512), (512, 512)]]
TILE_QB = [[None, 0, 4], [8, 12]]
C_CQ, C_CKV, C_KPE, C_CB, C_CC, C_CX, C_FQ, C_FK, C_FV, C_FL, C_G = 0, 512, 1024, 1088, 2112, 3136, 4160, 5184, 6208, 7232, 7240
G_MIXPRE, G_MIXPOST, G_FFNPRE, G_FFNPOST, G_QLAT, G_KVLAT, G_CONV, G_BF = 0, 16, 32, 48, 64, 68, 72, 96
NGV = 97
ENGS = ("pe", "act", "dve", "pool", "sp")
ND = 16
MAXC = 30000
SAME_SYNC = True
MASKVAL = -30000.0


class Prog:
    def __init__(self):
        self.ops = []; self.lastw = {}; self.readers = {}; self.pending = {}
        self.dma_since = []; self.last_on = {}

    def add(self, eng, fn, reads=(), writes=(), dma=False):
        i = len(self.ops); deps = set()
        if eng in self.pending:
            deps |= self.pending.pop(eng)
        for k in reads:
            w = self.lastw.get(k)
            if w is not None: deps.add(w)
        for k in writes:
            w = self.lastw.get(k)
            if w is not None: deps.add(w)
            deps.update(self.readers.get(k, ()))
        for k in reads: self.readers.setdefault(k, []).append(i)
        for k in writes:
            self.lastw[k] = i; self.readers[k] = []
        fd = []
        for d in deps:
            o = self.ops[d]
            if o[0] == eng and not o[2] and not dma and (eng == "pe" or not SAME_SYNC):
                continue
            fd.append(d); o[4] = True
        self.ops.append([eng, fn, dma, fd, False, None])
        if dma: self.dma_since.append(i)
        else: self.last_on[eng] = i
        return i

    def barrier(self):
        s = set(self.last_on.values()) | set(self.dma_since)
        self.pending = {e: set(s) | self.pending.get(e, set()) for e in ENGS}
        self.dma_since = []; self.lastw = {}; self.readers = {}

    def count(self):
        cnt = {e: 0 for e in ENGS}
        for op in self.ops:
            if not op[2] and op[4] and len(op) <= 6: cnt[op[0]] += 1
        return cnt

    def assign(self, esem, dsem, ccsem=()):
        cnt = {e: 0 for e in ENGS}; dcnt = {e: 0 for e in ENGS}; ncc = 0
        for op in self.ops:
            eng = op[0]
            if len(op) > 6:
                op[5] = (ccsem[ncc % len(ccsem)], ncc // len(ccsem) + 1, 1, None); ncc += 1
            elif op[2]:
                j = dcnt[eng]; dcnt[eng] += 1
                s = dsem[eng][j % ND]
                op[5] = (s, 16 * (j // ND + 1), 16, (s, 16 * (j // ND)) if j >= ND else None)
            elif op[4]:
                c = cnt[eng]; cnt[eng] += 1
                op[5] = (esem[eng][c // MAXC], c % MAXC + 1, 1, None)
        self.dcnt = dcnt

    def run(self, engname, e):
        waited = {}
        for op in self.ops:
            if op[0] != engname: continue
            need = {}
            for d in op[3]:
                s, v = self.ops[d][5][0], self.ops[d][5][1]
                if need.get(id(s), (None, 0))[1] < v: need[id(s)] = (s, v)
            sig = op[5]
            if sig is not None and sig[3] is not None:
                s, v = sig[3]
                if need.get(id(s), (None, 0))[1] < v: need[id(s)] = (s, v)
            for k, (s, v) in need.items():
                if waited.get(k, 0) < v:
                    e.wait_ge(s, v); waited[k] = v
            ins = op[1](e)
            if sig is not None:
                ins.then_inc(sig[0], sig[2])


PAIRS = [[0, 1], [2, 3], [4, 5], [6, 7]]
STOP = 0
TH = 1040
TILES8 = [(0, 16), (16, 512), (528, 512)]
QB8 = [None, 8, 12]
R_M = 1297


def build(nlayers=DEPTH):
    nc = bass.Bass("TRN2", target_bir_lowering=False)
    P = Prog()

    def din(name, shape):
        return nc.dram_tensor(name, list(shape), F32, kind="ExternalInput").ap()

    def dscr(name, shape, dt):
        return nc.dram_tensor(name, list(shape), dt).ap()

    xT = din("xT", [D, 1024]); metaT = din("metaT", [D, NMETA])
    w_in = din("w_in", [DEPTH, D, DIN]); w_uq = din("w_uq", [DEPTH, 512, 1536]); w_ukv = din("w_ukv", [DEPTH, 512, 2048])
    w_branch = din("w_branch", [DEPTH, 3, 1024, D]); w_out = din("w_out", [DEPTH, D, D])
    w_ffn_in = din("w_ffn_in", [DEPTH, D, 2 * DFF]); w_ffn_out = din("w_ffn_out", [DEPTH, DFF, D])
    gv_d = din("gv", [128, DEPTH, NGV]); cos_d = din("cosT", [64, TH]); sin_d = din("sinT", [64, TH])
    mask_d = din("masktri", [128, 128]); ident_d = din("ident", [128, 128]); pm_d = din("pm", [128, 4])
    outT = nc.dram_tensor("outT", [D, 1024], F32, kind="ExternalOutput").ap()
    hT = dscr("hT", [D, TH], F32)
    kS = [dscr("kS%d" % c, [512, TH], BF16) for c in range(4)]; kR = [dscr("kR%d" % c, [1024, TH], BF16) for c in range(4)]
    vS = [dscr("vS%d" % c, [4 * TH, 128], BF16) for c in range(4)]; vR = [dscr("vR%d" % c, [8 * TH, 128], BF16) for c in range(4)]
    peS = dscr("peS", [64, TH], BF16); peR = dscr("peR", [128, TH], BF16)
    mS = dscr("mS", [R_M, 8], F32); mR = dscr("mR", [2 * R_M, 8], F32)

    def sb(name, shape, dt):
        return nc.alloc_sbuf_tensor(name, list(shape), dt)

    YH_raw = sb("YH", [128, 16 * TH], BF16)
    YH = YH_raw[:, :].rearrange("p (c t) -> p c t", c=16)
    AR = sb("AR", [128, 49920], BF16)
    SL = [sb("slab%d" % i, [128, 4096], BF16) for i in range(3)]
    PH = sb("PH", [128, 7168], BF16)
    ATT = sb("ATT", [128, 5280], BF16)
    PT = [sb("pt%d" % i, [128, 512], BF16) for i in range(2)]
    COS = sb("cos", [64, TH], F32); SIN = sb("sin", [64, TH], F32)
    TMP = [sb("tmp%d" % i, [128, 512], F32) for i in range(3)]
    SQ = [sb("sq%d" % i, [128, 512], BF16) for i in range(2)]
    ONES32 = sb("ones32", [128, 512], F32)
    CNEG = sb("cneg", [128, 17, 8], F32)
    GV = sb("gvs", [128, DEPTH, NGV], F32)
    IDB = sb("idb", [128, 128], BF16); IDF = sb("idf", [128, 128], F32)
    MSK = sb("msk", [128, 128], BF16); ONESB = sb("onesb", [128, 128], BF16)
    EPSC = sb("epsc", [128, 1], F32); CARRY = sb("carry", [8, 1], F32); NEGB = sb("negb", [8, 1], F32)
    PM = sb("pmask", [128, 4], F32)
    BT = sb("bt", [128, 8, 2], F32); UM = sb("um", [128, 8, 2], F32); UT = sb("ut", [128, 8, 2], F32); GU = sb("gu", [128, 8, 2], F32)
    DC = sb("dc", [128, 2, 8], F32)
    GC = sb("gc", [8, 1], F32); CT = sb("ct", [128, 9, 8], F32); GCT = sb("gct", [128, 8, 8], F32)
    QR_t = sb("qr", [64, TH], BF16); QR = QR_t[:, :]
    PS = [nc.alloc_psum_tensor("ps%d" % i, [128, 512], F32) for i in range(8)]

    def arv(off, n, dt=BF16):
        a = AR[:, off:off + n]
        return a.bitcast(F32) if dt == F32 else a
    HST = arv(0, 16384, F32).rearrange("p (c t) -> p c t", c=16)
    CQN = arv(0, 4160).rearrange("p (c t) -> p c t", c=4)
    CKN = arv(4160, 4160).rearrange("p (c t) -> p c t", c=4)
    OA = arv(8320, 8320).rearrange("p (c t) -> p c t", c=8)
    OB = arv(16640, 8320).rearrange("p (c t) -> p c t", c=8)
    OC = arv(24960, 8320).rearrange("p (c t) -> p c t", c=8)
    MG = arv(33280, 16640).rearrange("p (c t) -> p c t", c=16)
    CFM = arv(33280, 2080, F32)[0:8, :]
    CRAW = arv(16640, 16640, F32).rearrange("p (c t) -> p c t", c=8)
    ACTB = arv(0, 45760).rearrange("p (c t) -> p c t", c=NFF)

    def wh(i):
        base = i * 2048
        return (PH[:, base:base + 768].rearrange("p (k c) -> p k c", k=4),
                PH[:, base + 768:base + 1024].rearrange("p (k c) -> p k c", k=4),
                PH[:, base + 1024:base + 2048].rearrange("p (k c) -> p k c", k=4))
    KPE = PH[0:64, 4096:4096 + 2064]
    CROW = PH[0:1, 0:2080].bitcast(F32)
    CHI = PH[0:1, 2080:3120]; CLO = PH[0:1, 3120:4160]
    KT = ATT[:, 0:2064]; VV = ATT[:, 2064:2064 + 2176].rearrange("p (b d) -> p b d", b=17); QN = ATT[:, 4240:5280]
    CVB = ATT[:, 0:1040]; CVU = ATT[:, 1040:1040 + 2084].bitcast(F32); CVA = ATT[:, 3124:3124 + 2080].bitcast(F32)

    def dv(pname, fn, reads=(), writes=()):
        return P.add(pname, fn, reads, writes)

    def dma(q, out, in_, reads=(), writes=()):
        return P.add(q, lambda e, o=out, i=in_: e.dma_start(out=o, in_=i), reads, writes, dma=True)

    def gather(snd, rcv, skey, rkey):
        i = P.add("pool", lambda e: e.collective_compute("AllGather", op=ALU.bypass, replica_groups=PAIRS,
                                                         ins=[snd.opt()], outs=[rcv.opt()]), [skey], [rkey])
        P.ops[i].append("cc")

    dma("sp", GV[:, :, :], gv_d, writes=["gv"])
    dma("sp", PM[:, :], pm_d, writes=["const"])
    dma("pool", IDB[:, :], ident_d, writes=["const"])
    dma("pool", MSK[:, :], mask_d, writes=["const"])
    dma("sp", IDF[:, :], ident_d, writes=["const"])
    dma("sp", COS[:, :], cos_d, writes=["rope"])
    dma("sp", SIN[:, :], sin_d, writes=["rope"])
    dv("dve", lambda e: e.memset(ONESB[:, :], 1.0), writes=["const"])
    dv("dve", lambda e: e.memset(ONES32[:, :], 1.0), writes=["const"])
    dv("dve", lambda e: e.memset(EPSC[:, :], EPS), writes=["const"])
    for (t0_, n_, src_) in [(0, NMETA, metaT[:, 0:NMETA])] + [(NMETA + 512 * i, 512, xT[:, 512 * i:512 * i + 512]) for i in range(2)]:
        dma("sp", HST[:, :, 0:n_], src_.rearrange("(c p) t -> p c t", p=128), writes=["hst"])
        dma("sp", hT[:, t0_:t0_ + n_].rearrange("(c p) t -> p c t", p=128), HST[:, :, 0:n_], reads=["hst"], writes=["hT"])
    P.barrier()

    bankctr = [0]; banklist = [[0, 1, 2, 3]]
    def nextbank():
        bl = banklist[0]
        b = bl[bankctr[0] % len(bl)]; bankctr[0] += 1
        return b
    slabctr = [0]

    def load_slab(wd, nk, kp, cols):
        i = slabctr[0] % 3; slabctr[0] += 1
        view = SL[i][0:kp, 0:nk * cols].rearrange("p (k c) -> p k c", k=nk)
        dma("pool", view, wd.rearrange("(k p) c -> p k c", p=kp), writes=[("slab", i)])
        return view, ("slab", i)

    def mm(out, lhsT, rhs, start, stop, reads, writes):
        P.add("pe", lambda e, o=out, l=lhsT, r=rhs, s=start, t=stop: e.matmul(o, l, r, start=s, stop=t), reads, writes)

    def linear(xk, xkeys, nk, kp, wd, ncols, tiles, epi, slabcols=256):
        for c0 in range(0, ncols, slabcols):
            cw = min(slabcols, ncols - c0)
            view, skey = load_slab(wd[:, c0:c0 + cw], nk, kp, cw)
            for mc in range(0, cw, 128):
                mw = min(128, cw - mc)
                for ti, (t0, n) in enumerate(tiles):
                    b = nextbank()
                    for k in range(nk):
                        mm(PS[b][0:mw, 0:n], view[:, k, mc:mc + mw], xk(k, t0, n), k == 0, k == nk - 1,
                           [skey] + xkeys, [("ps", b)])
                    epi(c0 + mc, mw, ti, t0, n, PS[b], ("ps", b))

    def act(out, in_, func, reads, writes, **kw):
        P.add("act", lambda e, o=out, i=in_, f=func, k=kw: e.activation(o, i, f, **k), reads, writes)

    def rstd_from(psb, pskey, n, tmp, tkey, inv_n):
        act(tmp[:, 0:n], psb[:, 0:n], AF.Ln, [pskey, "const"], [tkey], scale=inv_n, bias=EPSC[:, 0:1])
        act(tmp[:, 0:n], tmp[:, 0:n], AF.Exp, [tkey], [tkey], scale=-0.5)

    def kblock(kb):
        return (0, 16) if kb == 0 else (16 + 128 * (kb - 1), 128)

    def layer(l):
        if True:
            g0 = 0; tiles = TILES8; hf = 0
            hn = YH

            def norm_phase(gcol):
                for ti, (t0, n) in enumerate(tiles):
                    dma("sp", HST[:, :, 0:n], hT[:, t0:t0 + n].rearrange("(c p) t -> p c t", p=128), reads=["hT"], writes=["hst"])
                    for c in range(KC):
                        s = SQ[c % 2]
                        act(s[:, 0:n], HST[:, c, 0:n], AF.Square, ["hst"], [("sq", c % 2)])
                        mm(PS[4][:, 0:n], ONESB[:, :], s[:, 0:n], c == 0, c == KC - 1, [("sq", c % 2), "const"], [("ps", 4)])
                    rstd_from(PS[4], ("ps", 4), n, TMP[0], ("tmp", 0), 1.0 / D)
                    for c in range(KC):
                        P.add("dve", lambda e, c=c, t0=t0, n=n: e.scalar_tensor_tensor(
                            hn[:, c, t0:t0 + n], HST[:, c, 0:n], GV[:, l, gcol + c:gcol + c + 1], TMP[0][:, 0:n], ALU.mult, ALU.mult),
                            ["hst", ("tmp", 0), "gv"], ["hn"])

            def residual_phase(gcol, ssbank):
                for ti, (t0, n) in enumerate(tiles):
                    rstd_from(PS[ssbank + ti], ("ps", ssbank + ti), n, TMP[0], ("tmp", 0), 1.0 / D)
                    dma("sp", HST[:, :, 0:n], hT[:, t0:t0 + n].rearrange("(c p) t -> p c t", p=128), reads=["hT"], writes=["hst"])
                    for c in range(KC):
                        tb = TMP[1 + c % 2]; tk = ("tmp", 1 + c % 2)
                        P.add("dve", lambda e, c=c, t0=t0, n=n, tb=tb: e.tensor_scalar(
                            tb[:, 0:n], YH[:, c, t0:t0 + n], GV[:, l, gcol + c:gcol + c + 1], 0.0, ALU.mult, ALU.add),
                            ["Y", "gv"], [tk])
                        P.add("dve", lambda e, n=n, tb=tb: e.tensor_tensor(tb[:, 0:n], tb[:, 0:n], TMP[0][:, 0:n], ALU.mult),
                              [tk, ("tmp", 0)], [tk])
                        P.add("dve", lambda e, c=c, n=n, tb=tb: e.tensor_tensor(HST[:, c, 0:n], HST[:, c, 0:n], tb[:, 0:n], ALU.add),
                              ["hst", tk], ["hst"])
                    last = (l == nlayers - 1)
                    dma("sp", hT[:, t0:t0 + n].rearrange("(c p) t -> p c t", p=128), HST[:, :, 0:n], reads=["hst"], writes=["hT"])
                    if last and ti > 0:
                        o0 = t0 - NMETA
                        dma("sp", outT[:, o0:o0 + n].rearrange("(c p) t -> p c t", p=128), HST[:, :, 0:n], reads=["hst"], writes=["outT"])

            def y_epilogue(ssbank):
                def epi(col, mw, ti, t0, n, ps, pk):
                    c = col // 128
                    P.add("dve", lambda e, c=c, t0=t0, n=n, ps=ps: e.tensor_copy(YH[:, c, t0:t0 + n], ps[:, 0:n]), [pk], [("Y", c, ti)])
                    si = (c + ti) % 2; s = SQ[si]
                    act(s[:, 0:n], YH[:, c, t0:t0 + n], AF.Square, [("Y", c, ti)], [("sq", si)])
                    mm(PS[ssbank + ti][:, 0:n], ONESB[:, :], s[:, 0:n], c == 0, c == KC - 1, [("sq", si), "const"], [("ps", ssbank + ti)])
                return epi

            norm_phase(G_MIXPRE)
            P.barrier()
            hnk = lambda k, t0, n: hn[:, k, t0:t0 + n]

            def epi_lat(col, mw, ti, t0, n, ps, pk):
                c = col // 128
                P.add("dve", lambda e, c=c, t0=t0, n=n, ps=ps: e.tensor_copy(CRAW[:, c, t0:t0 + n], ps[:, 0:n]), [pk], ["craw"])
            linear(hnk, ["hn"], KC, 128, w_in[l][:, 0:1024], 1024, tiles, epi_lat)
            for which, (dst, gcol) in enumerate(((CQN, G_QLAT), (CKN, G_KVLAT))):
                for ti, (t0, n) in enumerate(tiles):
                    for c in range(4):
                        s = SQ[c % 2]
                        act(s[:, 0:n], CRAW[:, which * 4 + c, t0:t0 + n], AF.Square, ["craw"], [("sq", c % 2)])
                        mm(PS[4][:, 0:n], ONESB[:, :], s[:, 0:n], c == 0, c == 3, [("sq", c % 2), "const"], [("ps", 4)])
                    rstd_from(PS[4], ("ps", 4), n, TMP[0], ("tmp", 0), 1.0 / 512)
                    for c in range(4):
                        P.add("dve", lambda e, c=c, t0=t0, n=n, dst=dst, gcol=gcol, which=which: e.scalar_tensor_tensor(
                            dst[:, c, t0:t0 + n], CRAW[:, which * 4 + c, t0:t0 + n], GV[:, l, gcol + c:gcol + c + 1], TMP[0][:, 0:n], ALU.mult, ALU.mult),
                            ["craw", ("tmp", 0), "gv"], ["lat"])

            def rope_evac(psx, kx, psr, kr, t0, n, dst, dkey, scale):
                P.add("dve", lambda e: e.tensor_tensor(TMP[1][0:64, 0:n], psx[0:64, 0:n], COS[:, t0:t0 + n], ALU.mult), [kx, "rope"], [("tmp", 1)])
                P.add("dve", lambda e: e.tensor_tensor(TMP[2][0:64, 0:n], psr[0:64, 0:n], SIN[:, t0:t0 + n], ALU.mult), [kr, "rope"], [("tmp", 2)])
                if scale == 1.0:
                    P.add("dve", lambda e: e.tensor_tensor(dst, TMP[1][0:64, 0:n], TMP[2][0:64, 0:n], ALU.add), [("tmp", 1), ("tmp", 2)], [dkey])
                else:
                    P.add("dve", lambda e: e.tensor_tensor(TMP[1][0:64, 0:n], TMP[1][0:64, 0:n], TMP[2][0:64, 0:n], ALU.add), [("tmp", 1), ("tmp", 2)], [("tmp", 1)])
                    P.add("dve", lambda e: e.tensor_scalar(dst, TMP[1][0:64, 0:n], float(scale), 0.0, ALU.mult, ALU.add), [("tmp", 1)], [dkey])

            def own_col(t0):
                return t0 if t0 < 16 else 1024 + t0

            i0 = slabctr[0] % 3; slabctr[0] += 1
            kpw = SL[i0][:, 0:16 * 128].rearrange("p (k c) -> p k c", k=16)
            wsrc = w_in[l]
            dma("pool", kpw[:, :, 0:64], wsrc[:, C_KPE:C_KPE + 64].rearrange("(k p) c -> p k c", p=128), writes=[("slab", i0)])
            dma("pool", kpw[:, :, 64:96], wsrc[:, C_KPE + 32:C_KPE + 64].rearrange("(k p) c -> p k c", p=128), writes=[("slab", i0)])
            dma("pool", kpw[:, :, 96:128], wsrc[:, C_KPE:C_KPE + 32].rearrange("(k p) c -> p k c", p=128), writes=[("slab", i0)])
            for ti, (t0, n) in enumerate(tiles):
                b1 = nextbank(); b2 = nextbank()
                for k in range(KC):
                    mm(PS[b1][0:64, 0:n], kpw[:, k, 0:64], hnk(k, t0, n), k == 0, k == KC - 1, [("slab", i0), "hn"], [("ps", b1)])
                for k in range(KC):
                    mm(PS[b2][0:64, 0:n], kpw[:, k, 64:128], hnk(k, t0, n), k == 0, k == KC - 1, [("slab", i0), "hn"], [("ps", b2)])
                oc_ = own_col(t0)
                rope_evac(PS[b1], ("ps", b1), PS[b2], ("ps", b2), t0, n, KPE[:, oc_:oc_ + n], "kpe", 1.0)
            dma("sp", peS[:, 0:16], KPE[:, 0:16], reads=["kpe"], writes=["peS"])
            dma("sp", peS[:, 16:TH], KPE[:, 1040:2064], reads=["kpe"], writes=["peS"])

            def store_kv(a, h, ktile, ktkey):
                u = a * 8 + h; c = u // 4; r0 = (u % 4) * 128; v0 = (u % 4) * TH
                dma("sp", kS[c][r0:r0 + 128, :], ktile, reads=[ktkey], writes=[("kS", c)])
                dma("sp", vS[c][v0:v0 + 16, :], VV[0:16, 0, :], reads=["v"], writes=[("vS", c)])
                dma("sp", vS[c][v0 + 16:v0 + TH, :].rearrange("(b p) d -> p b d", p=128), VV[:, 1:9, :], reads=["v"], writes=[("vS", c)])

            for h in range(8):
                wq, wqs, wkv = wh(h % 2); wkey = ("wh", h % 2)
                dma("pool", wkv, w_ukv[l][:, h * 256:(h + 1) * 256].rearrange("(k p) c -> p k c", p=128), writes=[wkey])
                kst = KT[:, 0:TH] if h % 2 == 0 else QN[:, 0:TH]; kstk = ("kst", h % 2)
                for ti, (t0, n) in enumerate(tiles):
                    b = nextbank()
                    for k in range(4):
                        mm(PS[b][:, 0:n], wkv[:, k, 0:128], CKN[:, k, t0:t0 + n], k == 0, k == 3, [wkey, "lat"], [("ps", b)])
                    act(kst[:, t0:t0 + n], PS[b][:, 0:n], AF.Copy, [("ps", b)], [kstk])
                for kb in range(9):
                    k0, kn = kblock(kb)
                    b = nextbank()
                    for k in range(4):
                        mm(PS[b][0:kn, 0:128], CKN[:, k, k0:k0 + kn], wkv[:, k, 128:256], k == 0, k == 3, [wkey, "lat"], [("ps", b)])
                    act(VV[0:kn, kb, :], PS[b][0:kn, 0:128], AF.Copy, [("ps", b)], ["v"])
                store_kv(0, h, kst, kstk)

            for c in (0, 1):
                gather(kS[c], kR[c], ("kS", c), ("kR", c)); gather(vS[c], vR[c], ("vS", c), ("vR", c))
            gather(peS, peR, "peS", "peR")

            dv("dve", lambda e: e.memset(CARRY[:, :], 0.0), writes=["carry"])
            dv("dve", lambda e: e.tensor_scalar(NEGB[:, :], GV[0:8, l, G_BF:G_BF + 1], -1.0, 0.0, ALU.mult, ALU.add), reads=["gv"], writes=["negb"])
            view, skey = load_slab(w_in[l][:, C_FL:C_FL + 8], KC, 128, 8)
            for ti, (t0, n) in enumerate(tiles):
                b = nextbank()
                for k in range(KC):
                    mm(PS[b][0:8, 0:n], view[:, k, 0:8], hnk(k, t0, n), k == 0, k == KC - 1, [skey, "hn"], [("ps", b)])
                act(TMP[1][0:8, 0:n], PS[b][0:8, 0:n], AF.Exp, [("ps", b), "negb"], [("tmp", 1)], scale=-1.0, bias=NEGB[:, 0:1])
                act(TMP[1][0:8, 0:n], TMP[1][0:8, 0:n], AF.Ln, [("tmp", 1)], [("tmp", 1)], bias=1.0)
                P.add("dve", lambda e, t0=t0, n=n: e.tensor_tensor_scan(CFM[:, t0:t0 + n], ONES32[0:8, 0:n], TMP[1][0:8, 0:n], CARRY[:, 0:1], ALU.mult, ALU.subtract),
                      [("tmp", 1), "carry", "const"], ["cfm"])
                P.add("dve", lambda e, t0=t0, n=n: e.tensor_copy(CARRY[:, 0:1], CFM[:, t0 + n - 1:t0 + n]), ["cfm"], ["carry"])
            for kb in range(9):
                k0, kn = kblock(kb)
                b = nextbank()
                P.add("pe", lambda e, b=b, k0=k0, kn=kn: e.transpose(PS[b][0:kn, 0:8], CFM[:, k0:k0 + kn], IDF[0:8, 0:8]), ["cfm", "const"], [("ps", b)])
                P.add("dve", lambda e, b=b, kb=kb, kn=kn: e.tensor_copy(CT[0:kn, kb, :], PS[b][0:kn, 0:8]), [("ps", b)], ["ct"])
            dma("sp", mS[0:16, :], CT[0:16, 0, :], reads=["ct"], writes=["mS"])
            dma("sp", mS[16:TH, :].rearrange("(b p) h -> p b h", p=128), CT[:, 1:9, :], reads=["ct"], writes=["mS"])
            dma("sp", mS[1296:1297, :].rearrange("a h -> h a"), CFM[:, TH - 1:TH], reads=["cfm"], writes=["mS"])

            for h in range(8):
                vk, kk = load_slab(w_in[l][:, C_FK + 128 * h:C_FK + 128 * h + 128], KC, 128, 128)
                kst = KT[:, 0:TH] if h % 2 == 0 else QN[:, 0:TH]; kstk = ("kst", h % 2)
                for ti, (t0, n) in enumerate(tiles):
                    b = nextbank()
                    for k in range(KC):
                        mm(PS[b][:, 0:n], vk[:, k, :], hnk(k, t0, n), k == 0, k == KC - 1, [kk, "hn"], [("ps", b)])
                    act(kst[:, t0:t0 + n], PS[b][:, 0:n], AF.Copy, [("ps", b)], [kstk])
                vv, kvk = load_slab(w_in[l][:, C_FV + 128 * h:C_FV + 128 * h + 128], KC, 128, 128)
                for kb in range(9):
                    k0, kn = kblock(kb)
                    b = nextbank()
                    for k in range(KC):
                        mm(PS[b][0:kn, 0:128], hn[:, k, k0:k0 + kn], vv[:, k, :], k == 0, k == KC - 1, [kvk, "hn"], [("ps", b)])
                    act(VV[0:kn, kb, :], PS[b][0:kn, 0:128], AF.Copy, [("ps", b)], ["v"])
                store_kv(1, h, kst, kstk)
            P.barrier()
            for c in (2, 3):
                gather(kS[c], kR[c], ("kS", c), ("kR", c)); gather(vS[c], vR[c], ("vS", c), ("vR", c))

            for j in range(8):
                vb, kb_ = load_slab(w_in[l][:, C_CB + 128 * j:C_CB + 128 * j + 128], KC, 128, 128)
                vc, kc_ = load_slab(w_in[l][:, C_CC + 128 * j:C_CC + 128 * j + 128], KC, 128, 128)
                vx, kx_ = load_slab(w_in[l][:, C_CX + 128 * j:C_CX + 128 * j + 128], KC, 128, 128)
                dv("dve", lambda e: e.memset(CVU[:, 0:2], 0.0), writes=["cvu"])
                for ti, (t0, n) in enumerate(tiles):
                    b = nextbank()
                    for k in range(KC):
                        mm(PS[b][:, 0:n], vb[:, k, :], hnk(k, t0, n), k == 0, k == KC - 1, [kb_, "hn"], [("ps", b)])
                    act(CVB[:, t0:t0 + n], PS[b][:, 0:n], AF.Copy, [("ps", b)], ["cvb"])
                    if ti == 1:
                        P.add("dve", lambda e, b=b, j=j: e.tensor_copy(BT[:, j, :], PS[b][:, 0:2]), [("ps", b)], ["bt"])
                for ti, (t0, n) in enumerate(tiles):
                    b1 = nextbank()
                    for k in range(KC):
                        mm(PS[b1][:, 0:n], vc[:, k, :], hnk(k, t0, n), k == 0, k == KC - 1, [kc_, "hn"], [("ps", b1)])
                    act(TMP[ti][:, 0:n], PS[b1][:, 0:n], AF.Copy, [("ps", b1)], [("tmp", ti)])
                for ti, (t0, n) in enumerate(tiles):
                    b2 = nextbank()
                    for k in range(KC):
                        mm(PS[b2][:, 0:n], vx[:, k, :], hnk(k, t0, n), k == 0, k == KC - 1, [kx_, "hn"], [("ps", b2)])
                    P.add("dve", lambda e, b2=b2, t0=t0, n=n, ti=ti: e.tensor_tensor(CVU[:, 2 + t0:2 + t0 + n], PS[b2][:, 0:n], TMP[ti][:, 0:n], ALU.mult),
                          [("ps", b2), ("tmp", ti)], ["cvu"])
                gc = G_CONV + 3 * j
                dv("dve", lambda e, gc=gc: e.tensor_scalar(CVA[:, 0:TH], CVU[:, 2:2 + TH], GV[:, l, gc + 2:gc + 3], 0.0, ALU.mult, ALU.add), reads=["cvu", "gv"], writes=["cva"])
                dv("dve", lambda e, gc=gc: e.scalar_tensor_tensor(CVA[:, 0:TH], CVU[:, 1:1 + TH], GV[:, l, gc + 1:gc + 2], CVA[:, 0:TH], ALU.mult, ALU.add), reads=["cvu", "cva", "gv"], writes=["cva"])
                dv("dve", lambda e, gc=gc: e.scalar_tensor_tensor(CVA[:, 0:TH], CVU[:, 0:TH], GV[:, l, gc:gc + 1], CVA[:, 0:TH], ALU.mult, ALU.add), reads=["cvu", "cva", "gv"], writes=["cva"])
                dv("dve", lambda e, j=j: e.tensor_tensor(OB[:, j, 0:TH], CVA[:, 0:TH], CVB[:, 0:TH], ALU.mult), reads=["cva", "cvb"], writes=["ob"])
                dv("dve", lambda e, j=j: e.tensor_copy(UM[:, j, :], CVU[:, 16:18]), reads=["cvu"], writes=["um"])
                dv("dve", lambda e, j=j: e.tensor_copy(UT[:, j, :], CVU[:, TH:TH + 2]), reads=["cvu"], writes=["ut"])
            dma("sp", mS[1040:1296, :].rearrange("(p a) b -> p (a b)", a=2), UT[:, :, :].rearrange("p j k -> p (j k)"), reads=["ut"], writes=["mS"])

            if STOP == 1: return
            gather(mS, mR, "mS", "mR")
            P.barrier()

            if STOP == 2: return
            dma("sp", KPE[:, 16:1040], peR[0:64, 16:TH], reads=["peR"], writes=["kpe"])
            dma("sp", GC[:, :], mR[1296:1297, :].rearrange("a h -> h a"), reads=["mR"], writes=["gc"])
            dma("sp", GCT[:, :, :], mR[16:TH, :].rearrange("(b p) h -> p b h", p=128), reads=["mR"], writes=["gct"])
            dma("sp", GU[:, :, :].rearrange("p j k -> p (j k)"), mR[1040:1296, :].rearrange("(p a) b -> p (a b)", a=2), reads=["mR"], writes=["gu"])
            dv("dve", lambda e: e.tensor_tensor(GC[:, :], GC[:, :], CFM[:, 15:16], ALU.subtract), reads=["gc", "cfm"], writes=["gc"])
            dv("dve", lambda e: e.tensor_tensor(GC[:, :], GC[:, :], PM[0:8, 1:2], ALU.mult), reads=["gc", "const"], writes=["gc"])
            dv("dve", lambda e: e.tensor_scalar(CFM[:, 16:TH], CFM[:, 16:TH], GC[:, 0:1], 0.0, ALU.add, ALU.add), reads=["gc", "cfm"], writes=["cfm"])
            for kb in range(9):
                k0, kn = kblock(kb)
                b = nextbank()
                P.add("pe", lambda e, b=b, k0=k0, kn=kn: e.transpose(PS[b][0:kn, 0:8], CFM[:, k0:k0 + kn], IDF[0:8, 0:8]), ["cfm", "const"], [("ps", b)])
                dst = 0 if kb == 0 else kb + 8
                P.add("dve", lambda e, b=b, dst=dst, kn=kn: e.tensor_scalar(CNEG[0:kn, dst, :], PS[b][0:kn, 0:8], -1.0, 0.0, ALU.mult, ALU.add), [("ps", b)], ["cneg"])
            dv("dve", lambda e: e.tensor_scalar(CNEG[:, 1:9, :], GCT[:, :, :], -1.0, PM[:, 0:1], ALU.mult, ALU.add), reads=["gct", "const"], writes=["cneg"])
            dv("dve", lambda e: e.tensor_tensor(GU[:, :, :], GU[:, :, :], UM[:, :, :], ALU.subtract), reads=["gu", "um"], writes=["gu"])
            dv("dve", lambda e: e.tensor_scalar(GU[:, :, :], GU[:, :, :], PM[:, 1:2], 0.0, ALU.mult, ALU.add), reads=["gu", "const"], writes=["gu"])
            CW = GV[:, l, G_CONV:G_CONV + 24].rearrange("p (j k) -> p j k", k=3)
            dv("dve", lambda e: e.tensor_tensor(DC[:, 0, :], CW[:, :, 1], GU[:, :, 1], ALU.mult), reads=["gu", "gv"], writes=["dc"])
            dv("dve", lambda e: e.tensor_tensor(DC[:, 1, :], CW[:, :, 0], GU[:, :, 0], ALU.mult), reads=["gu", "gv"], writes=["dc"])
            dv("dve", lambda e: e.tensor_tensor(DC[:, 0, :], DC[:, 0, :], DC[:, 1, :], ALU.add), reads=["dc"], writes=["dc"])
            dv("dve", lambda e: e.tensor_tensor(DC[:, 1, :], CW[:, :, 0], GU[:, :, 1], ALU.mult), reads=["gu", "gv", "dc"], writes=["dc"])
            dv("dve", lambda e: e.tensor_tensor(DC[:, 0, :], DC[:, 0, :], BT[:, :, 0], ALU.mult), reads=["dc", "bt"], writes=["dc"])
            dv("dve", lambda e: e.tensor_tensor(DC[:, 1, :], DC[:, 1, :], BT[:, :, 1], ALU.mult), reads=["dc", "bt"], writes=["dc"])
            dv("dve", lambda e: e.tensor_tensor(OB[:, :, 16], OB[:, :, 16], DC[:, 0, :], ALU.add), reads=["dc", "ob"], writes=["ob"])
            dv("dve", lambda e: e.tensor_tensor(OB[:, :, 17], OB[:, :, 17], DC[:, 1, :], ALU.add), reads=["dc", "ob"], writes=["ob"])

            if STOP == 3: return
            def load_kv(a, h):
                u = a * 8 + h; c = u // 4; r0 = (u % 4) * 128; v0 = (u % 4) * TH
                dma("sp", KT[:, 0:16], kS[c][r0:r0 + 128, 0:16], reads=[("kS", c)], writes=["kt"])
                dma("sp", KT[:, 16:1040], kR[c][r0:r0 + 128, 16:TH], reads=[("kR", c)], writes=["kt"])
                dma("sp", KT[:, 1040:2064], kS[c][r0:r0 + 128, 16:TH], reads=[("kS", c)], writes=["kt"])
                dma("sp", VV[0:16, 0, :], vS[c][v0:v0 + 16, :], reads=[("vS", c)], writes=["v"])
                dma("sp", VV[:, 1:9, :], vR[c][v0 + 16:v0 + TH, :].rearrange("(b p) d -> p b d", p=128), reads=[("vR", c)], writes=["v"])
                dma("sp", VV[:, 9:17, :], vS[c][v0 + 16:v0 + TH, :].rearrange("(b p) d -> p b d", p=128), reads=[("vS", c)], writes=["v"])

            def attention(h, Oview, okey, qparts, kparts, fox):
                for ti, (t0, n) in enumerate(tiles):
                    qb0 = QB8[ti]
                    kbs = [0] if qb0 is None else [kb for kb in range(17) if kb == 0 or (kb - 1) <= qb0 + 3]
                    for ii, kb in enumerate(kbs):
                        k0, kn = kblock(kb)
                        diag = False
                        if qb0 is None:
                            c0 = 0; diag = True
                        elif kb == 0 or (kb - 1) < qb0:
                            c0 = 0
                        else:
                            c0 = (kb - 1 - qb0) * 128; diag = True
                        nn = n - c0
                        sbk = 4 + (ii % 2); spk = ("ps", sbk); S = PS[sbk]
                        nparts = len(qparts) + (2 if fox else 0) + (1 if diag else 0)
                        j = 0
                        for (qf, kf) in zip(qparts, kparts):
                            mm(S[0:kn, 0:nn], kf(k0, kn), qf(t0 + c0, nn), j == 0, j == nparts - 1, ["kt", "q", "kpe"], [spk]); j += 1
                        if fox:
                            mm(S[0:kn, 0:nn], ONESB[0:1, 0:kn], CHI[:, t0 + c0:t0 + c0 + nn], False, j == nparts - 1, ["crow", "const"], [spk]); j += 1
                            mm(S[0:kn, 0:nn], ONESB[0:1, 0:kn], CLO[:, t0 + c0:t0 + c0 + nn], False, j == nparts - 1, ["crow", "const"], [spk]); j += 1
                        if diag:
                            dn = min(kn, nn)
                            mm(S[0:kn, 0:dn], IDB[0:kn, 0:kn], MSK[0:kn, 0:dn], False, True, ["const"], [spk]); j += 1
                        pt = PT[ii % 2]; ptk = ("pt", ii % 2)
                        partner = 1 <= kb <= 8
                        if fox:
                            act(pt[0:kn, 0:nn], S[0:kn, 0:nn], AF.Exp, [spk, "cneg"], [ptk], bias=CNEG[0:kn, kb, h:h + 1])
                        elif partner:
                            act(pt[0:kn, 0:nn], S[0:kn, 0:nn], AF.Exp, [spk, "const"], [ptk], bias=PM[0:kn, 0:1])
                        else:
                            act(pt[0:kn, 0:nn], S[0:kn, 0:nn], AF.Exp, [spk], [ptk])
                        first = ii == 0; lastk = ii == len(kbs) - 1
                        mm(PS[6][:, c0:n], VV[0:kn, kb, :], pt[0:kn, 0:nn], first, lastk, [ptk, "v"], [("ps", 6)])
                        mm(PS[7][:, c0:n], ONESB[0:kn, :], pt[0:kn, 0:nn], first, lastk, [ptk, "const"], [("ps", 7)])
                    P.add("dve", lambda e, n=n: e.reciprocal(TMP[0][:, 0:n], PS[7][:, 0:n]), [("ps", 7)], [("tmp", 0)])
                    P.add("dve", lambda e, t0=t0, n=n: e.tensor_tensor(Oview[:, h, t0:t0 + n], PS[6][:, 0:n], TMP[0][:, 0:n], ALU.mult),
                          [("ps", 6), ("tmp", 0)], [okey])

            SC_A = 192.0 ** -0.5
            for h in range(8):
                wq, wqs, wkv = wh(h % 2); wkey = ("wh", h % 2)
                dma("pool", wq, w_uq[l][:, h * 192:(h + 1) * 192].rearrange("(k p) c -> p k c", p=128), writes=[wkey])
                dma("pool", wqs[:, :, 0:32], w_uq[l][:, h * 192 + 160:h * 192 + 192].rearrange("(k p) c -> p k c", p=128), writes=[wkey])
                dma("pool", wqs[:, :, 32:64], w_uq[l][:, h * 192 + 128:h * 192 + 160].rearrange("(k p) c -> p k c", p=128), writes=[wkey])
                load_kv(0, h)
                for ti, (t0, n) in enumerate(tiles):
                    b = nextbank()
                    for k in range(4):
                        mm(PS[b][:, 0:n], wq[:, k, 0:128], CQN[:, k, t0:t0 + n], k == 0, k == 3, [wkey, "lat"], [("ps", b)])
                    P.add("dve", lambda e, b=b, t0=t0, n=n: e.tensor_scalar(QN[:, t0:t0 + n], PS[b][:, 0:n], SC_A, 0.0, ALU.mult, ALU.add), [("ps", b)], ["q"])
                    b1 = nextbank(); b2 = nextbank()
                    for k in range(4):
                        mm(PS[b1][0:64, 0:n], wq[:, k, 128:192], CQN[:, k, t0:t0 + n], k == 0, k == 3, [wkey, "lat"], [("ps", b1)])
                    for k in range(4):
                        mm(PS[b2][0:64, 0:n], wqs[:, k, :], CQN[:, k, t0:t0 + n], k == 0, k == 3, [wkey, "lat"], [("ps", b2)])
                    rope_evac(PS[b1], ("ps", b1), PS[b2], ("ps", b2), t0, n, QR[:, t0:t0 + n], "q", SC_A)
                attention(h, OA, "oa",
                          [lambda t, n: QN[:, t:t + n], lambda t, n: QR[:, t:t + n]],
                          [lambda k0, kn: KT[:, k0:k0 + kn], lambda k0, kn: KPE[:, k0:k0 + kn]], False)
            P.barrier()
            if STOP == 4: return

            SC_C = 128.0 ** -0.5
            for h in range(8):
                dma("sp", CROW[:, 0:TH], CFM[h:h + 1, 0:TH], reads=["cfm"], writes=["crow32"])
                dv("dve", lambda e: e.tensor_copy(CHI[:, 0:TH], CROW[:, 0:TH]), reads=["crow32"], writes=["crow"])
                dv("dve", lambda e: e.tensor_tensor(CROW[:, 0:TH], CROW[:, 0:TH], CHI[:, 0:TH], ALU.subtract), reads=["crow32", "crow"], writes=["crow32"])
                dv("dve", lambda e: e.tensor_copy(CLO[:, 0:TH], CROW[:, 0:TH]), reads=["crow32"], writes=["crow"])
                load_kv(1, h)
                vq, kq = load_slab(w_in[l][:, C_FQ + 128 * h:C_FQ + 128 * h + 128], KC, 128, 128)
                for ti, (t0, n) in enumerate(tiles):
                    b = nextbank()
                    for k in range(KC):
                        mm(PS[b][:, 0:n], vq[:, k, :], hnk(k, t0, n), k == 0, k == KC - 1, [kq, "hn"], [("ps", b)])
                    P.add("dve", lambda e, b=b, t0=t0, n=n: e.tensor_scalar(QN[:, t0:t0 + n], PS[b][:, 0:n], SC_C, 0.0, ALU.mult, ALU.add), [("ps", b)], ["q"])
                attention(h, OC, "oc", [lambda t, n: QN[:, t:t + n]], [lambda k0, kn: KT[:, k0:k0 + kn]], True)

            P.barrier()
            ACC = [ATT[:, i * 1024:(i + 1) * 1024].bitcast(F32) for i in range(3)]
            OBR = (OA, OB, OC); OKEY = ("oa", "ob", "oc")
            for m in range(16):
                for nb in range(3):
                    bview, bkey = load_slab(w_branch[l, nb][:, m * 128:m * 128 + 128], 8, 128, 128)
                    gview, gkey = load_slab(w_in[l][:, C_G + nb * D + m * 128:C_G + nb * D + m * 128 + 128], KC, 128, 128)
                    bys = []
                    for ti, (t0, n) in enumerate(tiles):
                        by = nextbank(); bys.append(by)
                        for k in range(8):
                            mm(PS[by][:, 0:n], bview[:, k, :], OBR[nb][:, k, t0:t0 + n], k == 0, k == 7, [bkey, OKEY[nb]], [("ps", by)])
                    for ti, (t0, n) in enumerate(tiles):
                        by = bys[ti]
                        bg = nextbank()
                        for k in range(KC):
                            mm(PS[bg][:, 0:n], gview[:, k, :], hnk(k, t0, n), k == 0, k == KC - 1, [gkey, "hn"], [("ps", bg)])
                        act(TMP[1][:, 0:n], PS[bg][:, 0:n], AF.Sigmoid, [("ps", bg)], [("tmp", 1)])
                        acc = ACC[ti]; ak = ("acc", ti)
                        if nb == 0:
                            P.add("dve", lambda e, by=by, n=n, acc=acc: e.tensor_tensor(acc[:, 0:n], PS[by][:, 0:n], TMP[1][:, 0:n], ALU.mult),
                                  [("ps", by), ("tmp", 1)], [ak])
                        else:
                            P.add("dve", lambda e, by=by, n=n: e.tensor_tensor(TMP[2][:, 0:n], PS[by][:, 0:n], TMP[1][:, 0:n], ALU.mult),
                                  [("ps", by), ("tmp", 1)], [("tmp", 2)])
                            if nb == 1:
                                P.add("dve", lambda e, n=n, acc=acc: e.tensor_tensor(acc[:, 0:n], acc[:, 0:n], TMP[2][:, 0:n], ALU.add),
                                      [ak, ("tmp", 2)], [ak])
                            else:
                                P.add("dve", lambda e, m=m, t0=t0, n=n, acc=acc: e.tensor_tensor(MG[:, m, t0:t0 + n], acc[:, 0:n], TMP[2][:, 0:n], ALU.add),
                                      [ak, ("tmp", 2)], ["mg"])
            P.barrier()

            linear(lambda k, t0, n: MG[:, k, t0:t0 + n], ["mg"], KC, 128, w_out[l], D, tiles, y_epilogue(4))
            P.barrier()
            residual_phase(G_MIXPOST, 4)
            P.barrier()

            norm_phase(G_FFNPRE)
            P.barrier()
            for j in range(NFF):
                gsl, gk = load_slab(w_ffn_in[l][:, j * 128:j * 128 + 128], KC, 128, 128)
                usl, uk = load_slab(w_ffn_in[l][:, DFF + j * 128:DFF + j * 128 + 128], KC, 128, 128)
                for ti, (t0, n) in enumerate(tiles):
                    bg = nextbank()
                    for k in range(KC):
                        mm(PS[bg][:, 0:n], gsl[:, k, :], hnk(k, t0, n), k == 0, k == KC - 1, [gk, "hn"], [("ps", bg)])
                    act(TMP[ti][:, 0:n], PS[bg][:, 0:n], AF.Silu, [("ps", bg)], [("tmp", ti)])
                for ti, (t0, n) in enumerate(tiles):
                    bu = nextbank()
                    for k in range(KC):
                        mm(PS[bu][:, 0:n], usl[:, k, :], hnk(k, t0, n), k == 0, k == KC - 1, [uk, "hn"], [("ps", bu)])
                    P.add("dve", lambda e, j=j, bu=bu, t0=t0, n=n, ti=ti: e.tensor_tensor(ACTB[:, j, t0:t0 + n], PS[bu][:, 0:n], TMP[ti][:, 0:n], ALU.mult),
                          [("ps", bu), ("tmp", ti)], ["actb"])
            P.barrier()
            yep = y_epilogue(4)
            for m in range(16):
                s1, k1 = load_slab(w_ffn_out[l][0:2816, m * 128:m * 128 + 128], 22, 128, 128)
                s2, k2 = load_slab(w_ffn_out[l][2816:5632, m * 128:m * 128 + 128], 22, 128, 128)
                for ti, (t0, n) in enumerate(tiles):
                    b = nextbank()
                    for k in range(NFF):
                        sv, sk = (s1, k1) if k < 22 else (s2, k2)
                        mm(PS[b][:, 0:n], sv[:, k % 22, :], ACTB[:, k, t0:t0 + n], k == 0, k == NFF - 1, [sk, "actb"], [("ps", b)])
                    yep(m * 128, 128, ti, t0, n, PS[b], ("ps", b))
            P.barrier()
            residual_phase(G_FFNPOST, 4)
            P.barrier()


    for l_ in range(nlayers):
        layer(l_)

    cnt = P.count()
    ncc = sum(1 for op in P.ops if len(op) > 6)
    with ExitStack() as es:
        esem = {e: [es.enter_context(nc.semaphore("s_%s_%d" % (e, i))) for i in range(cnt[e] // MAXC + 1)] for e in ENGS}
        dsem = {e: [es.enter_context(nc.semaphore("d_%s_%d" % (e, i))) for i in range(ND)] for e in ("sp", "pool")}
        ccsem = [es.enter_context(nc.semaphore("cc_%d" % i)) for i in range(min(ncc, 10))]
        for e in ("pe", "act", "dve"): dsem[e] = []
        P.assign(esem, dsem, ccsem)
        block = es.enter_context(nc.Block())

        @block.tensor
        def _(e): P.run("pe", e)

        @block.scalar
        def _(e): P.run("act", e)

        @block.vector
        def _(e): P.run("dve", e)

        @block.gpsimd
        def _(e): P.run("pool", e)

        @block.sync
        def _(e):
            P.run("sp", e)
            for q in ("sp", "pool"):
                tot = P.dcnt[q]
                for i in range(min(ND, tot)):
                    nfin = (tot - i + ND - 1) // ND
                    e.wait_ge(dsem[q][i], 16 * nfin)
    return nc


def _host_consts():
    inv_freq = (1.0 / (10000.0 ** (np.arange(0, 64, 2, dtype=np.float32) / np.float32(64)))).astype(np.float32)
    ang = np.arange(T, dtype=np.float32)[:, None] * inv_freq[None, :]
    c = np.cos(ang).astype(np.float32).T; s = np.sin(ang).astype(np.float32).T
    cosT = np.concatenate([c, c], 0); sinT = np.concatenate([-s, s], 0)
    k = np.arange(128)[:, None]; q = np.arange(128)[None, :]
    masktri = np.where(k > q, MASKVAL, 0.0).astype(np.float32)
    ident = np.eye(128, dtype=np.float32)
    return cosT, sinT, masktri, ident


_NC_CACHE = {}


def kernel(x, meta, w_in, b_forget, g_q_lat, g_kv_lat, w_uq, w_ukv, conv_w, w_branch, w_out,
           w_ffn_in, w_ffn_out, g_mix_pre, g_mix_post, g_ffn_pre, g_ffn_post, _nlayers=DEPTH, _cores=None):
    f = lambda a: np.ascontiguousarray(np.asarray(a, dtype=np.float32))
    x = f(x); meta = f(meta)
    gv = np.zeros((128, DEPTH, NGV), np.float32)
    def fm(a, nch):
        return np.transpose(np.asarray(a, np.float32).reshape(DEPTH, nch, 128), (2, 0, 1))
    gv[:, :, G_MIXPRE:G_MIXPRE + 16] = fm(g_mix_pre, 16); gv[:, :, G_MIXPOST:G_MIXPOST + 16] = fm(g_mix_post, 16)
    gv[:, :, G_FFNPRE:G_FFNPRE + 16] = fm(g_ffn_pre, 16); gv[:, :, G_FFNPOST:G_FFNPOST + 16] = fm(g_ffn_post, 16)
    gv[:, :, G_QLAT:G_QLAT + 4] = fm(g_q_lat, 4); gv[:, :, G_KVLAT:G_KVLAT + 4] = fm(g_kv_lat, 4)
    cw = np.asarray(conv_w, np.float32).reshape(DEPTH, 3, 8, 128)
    gv[:, :, G_CONV:G_CONV + 24] = np.transpose(cw, (3, 0, 2, 1)).reshape(128, DEPTH, 24)
    gv[0:8, :, G_BF] = np.asarray(b_forget, np.float32).T
    cosT, sinT, masktri, ident = _host_consts()
    global PAIRS
    cores = list(range(8)) if _cores is None else _cores
    PAIRS = [[2 * i, 2 * i + 1] for i in range(len(cores) // 2)]
    ck = (_nlayers, len(cores))
    if ck not in _NC_CACHE:
        _NC_CACHE[ck] = build(_nlayers)
    nc = _NC_CACHE[ck]
    common = dict(metaT=np.ascontiguousarray(meta.T), w_in=f(w_in), w_uq=f(w_uq), w_ukv=f(w_ukv), w_branch=f(w_branch),
                  w_out=f(w_out), w_ffn_in=f(w_ffn_in), w_ffn_out=f(w_ffn_out), gv=gv, masktri=masktri, ident=ident)
    in_maps = []
    for c in range(8):
        b, s = c // 2, c % 2
        m = dict(common)
        m["xT"] = np.ascontiguousarray(x[b, s * 1024:(s + 1) * 1024].T)
        pos = np.concatenate([np.arange(16), 16 + s * 1024 + np.arange(1024)])
        m["cosT"] = np.ascontiguousarray(cosT[:, pos]); m["sinT"] = np.ascontiguousarray(sinT[:, pos])
        pm = np.zeros((128, 4), np.float32)
        pm[:, 0] = 0.0 if s == 1 else MASKVAL
        pm[:, 1] = float(s)
        m["pm"] = pm
        in_maps.append(m)
    res = run_bass_kernel_spmd(nc, [in_maps[c] for c in cores], core_ids=list(range(len(cores))))
    out = np.zeros((4, SEQ, D), np.float32)
    for i, c in enumerate(cores):
        b, s = c // 2, c % 2
        out[b, s * 1024:(s + 1) * 1024] = np.asarray(res.results[i]["outT"]).T
    return out
```
